# Optimizing a Trainium2 kernel written in Bass

```python
import jax, jax.numpy as jnp
from jax import lax
import numpy as np

D_MODEL = 1024
BATCH = 4
SEQ = 8192
DEPTH = 1

N_META = 16
BLOCK = 128
PAD = BLOCK - N_META
HEAD_DIM = 64
SB_HEADS = 8
SB_WIDTH = SB_HEADS * HEAD_DIM
SWA_Q_HEADS = 16
SWA_KV_HEADS = 2
SWA_GROUP = SWA_Q_HEADS // SWA_KV_HEADS
SWA_WIDTH = SWA_Q_HEADS * HEAD_DIM
SWA_KV_WIDTH = SWA_KV_HEADS * HEAD_DIM
WINDOW = 128
ROPE_THETA = 10000.0
RMS_EPS = 1e-6
SPLITS = (SB_WIDTH, SB_WIDTH, SB_WIDTH, SWA_WIDTH, SWA_KV_WIDTH, SWA_KV_WIDTH, SB_WIDTH, SWA_WIDTH, D_MODEL, D_MODEL)
IN_COLS = sum(SPLITS)

kernel_name = "hybrid_stickbreak_swa_sink_gated"


def rms_norm(x, g):
    xf = x.astype(jnp.float32)
    y = xf * lax.rsqrt(jnp.mean(xf * xf, axis=-1, keepdims=True) + RMS_EPS)
    return (y * g.astype(jnp.float32)).astype(x.dtype)


def rope(x, pos):
    half = HEAD_DIM // 2
    inv = ROPE_THETA ** (-jnp.arange(half, dtype=jnp.float32) / half)
    ang = pos.astype(jnp.float32)[:, None] * inv[None, :]
    cos = jnp.cos(ang)[None, :, None, :]
    sin = jnp.sin(ang)[None, :, None, :]
    x1 = x[..., :half].astype(jnp.float32)
    x2 = x[..., half:].astype(jnp.float32)
    out = jnp.concatenate([x1 * cos - x2 * sin, x2 * cos + x1 * sin], axis=-1)
    return out.astype(x.dtype)


def stick_breaking_attention(q, k, v, valid):
    B, L, H, d = q.shape
    scale = d ** -0.5
    outs = []
    for blk in range(L // BLOCK):
        q0 = blk * BLOCK
        end = q0 + BLOCK
        z = jnp.einsum('bqhd,bkhd->bhqk', q[:, q0:end], k[:, :end],
                       preferred_element_type=jnp.float32) * scale
        t = q0 + jnp.arange(BLOCK)
        s = jnp.arange(end)
        mask = (s[None, :] < t[:, None]) & valid[None, :end]
        log_beta = jax.nn.log_sigmoid(z)
        log_1m = jnp.where(mask, log_beta - z, 0.0)
        rev = lax.cumsum(log_1m, axis=3, reverse=True)
        suffix = jnp.concatenate([rev[..., 1:], jnp.zeros_like(rev[..., :1])], axis=-1)
        w = jnp.where(mask, jnp.exp(log_beta + suffix), 0.0)
        outs.append(jnp.einsum('bhqk,bkhd->bqhd', w.astype(v.dtype), v[:, :end]))
    return jnp.concatenate(outs, axis=1)


def sliding_window_sink_attention(q, k, v, sinks, valid):
    B, L, Hq, d = q.shape
    nb = L // BLOCK
    scale = d ** -0.5
    qb = q.reshape(B, nb, BLOCK, SWA_KV_HEADS, SWA_GROUP, d)

    def band(t):
        tb = t.reshape(B, nb, BLOCK, SWA_KV_HEADS, d)
        prev = jnp.pad(tb[:, :-1], ((0, 0), (1, 0), (0, 0), (0, 0), (0, 0)))
        return jnp.concatenate([prev, tb], axis=2)

    kb, vb = band(k), band(v)
    vblk = valid.reshape(nb, BLOCK)
    kvalid = jnp.concatenate([jnp.pad(vblk[:-1], ((1, 0), (0, 0)), constant_values=False), vblk], axis=1)
    scores = jnp.einsum('bnqhgd,bnkhd->bnhgqk', qb, kb,
                        preferred_element_type=jnp.float32) * scale
    diff = (BLOCK + jnp.arange(BLOCK))[:, None] - jnp.arange(2 * BLOCK)[None, :]
    mask = ((diff >= 0) & (diff < WINDOW))[None] & kvalid[:, None, :]
    scores = jnp.where(mask[None, :, None, None], scores, -jnp.inf)
    sink = sinks.astype(jnp.float32).reshape(SWA_KV_HEADS, SWA_GROUP)[None, None, :, :, None, None]
    sink = jnp.broadcast_to(sink, scores.shape[:-1] + (1,))
    probs = jax.nn.softmax(jnp.concatenate([scores, sink], axis=-1), axis=-1)[..., :-1]
    o = jnp.einsum('bnhgqk,bnkhd->bnqhgd', probs.astype(v.dtype), vb)
    return o.reshape(B, L, Hq * d)


def setup_inputs(seed: int = 0) -> dict:
    key = jax.random.key(seed)
    ks = jax.random.split(key, 10)
    f32 = jnp.float32
    x = jax.random.normal(ks[0], (BATCH, SEQ, D_MODEL), f32)
    meta_tokens = jax.random.normal(ks[1], (N_META, D_MODEL), f32)
    norm_gain = 1.0 + 0.02 * jax.random.normal(ks[2], (DEPTH, D_MODEL), f32)
    w_in = jax.random.normal(ks[3], (DEPTH, D_MODEL, IN_COLS), f32) * D_MODEL ** -0.5
    w_branch_sb = jax.random.normal(ks[4], (DEPTH, SB_WIDTH, D_MODEL), f32) * SB_WIDTH ** -0.5
    w_branch_swa = jax.random.normal(ks[5], (DEPTH, SWA_WIDTH, D_MODEL), f32) * SWA_WIDTH ** -0.5
    w_out = jax.random.normal(ks[6], (DEPTH, D_MODEL, D_MODEL), f32) * D_MODEL ** -0.5
    attn_sinks = jax.random.normal(ks[7], (DEPTH, SWA_Q_HEADS), f32)
    final_norm_gain = 1.0 + 0.02 * jax.random.normal(ks[8], (D_MODEL,), f32)
    return {"x": x, "meta_tokens": meta_tokens, "norm_gain": norm_gain, "w_in": w_in,
            "w_branch_sb": w_branch_sb, "w_branch_swa": w_branch_swa, "w_out": w_out,
            "attn_sinks": attn_sinks, "final_norm_gain": final_norm_gain}


def reference(x, meta_tokens, norm_gain, w_in, w_branch_sb, w_branch_swa, w_out, attn_sinks, final_norm_gain):
    B = x.shape[0]
    meta = jnp.broadcast_to(meta_tokens[None].astype(x.dtype), (B, N_META, D_MODEL))
    pad = jnp.zeros((B, PAD, D_MODEL), x.dtype)
    h = jnp.concatenate([pad, meta, x], axis=1)
    L = h.shape[1]
    idx = jnp.arange(L)
    valid = idx >= PAD
    pos = idx - PAD
    offsets = [int(o) for o in np.cumsum(SPLITS)[:-1]]
    for l in range(DEPTH):
        xn = rms_norm(h, norm_gain[l])
        proj = xn @ w_in[l]
        sb_q, sb_k, sb_v, sw_q, sw_k, sw_v, sb_z, sw_z, g_sb, g_sw = jnp.split(proj, offsets, axis=-1)
        o_sb = stick_breaking_attention(sb_q.reshape(B, L, SB_HEADS, HEAD_DIM),
                                        sb_k.reshape(B, L, SB_HEADS, HEAD_DIM),
                                        sb_v.reshape(B, L, SB_HEADS, HEAD_DIM), valid).reshape(B, L, SB_WIDTH)
        q = rope(sw_q.reshape(B, L, SWA_Q_HEADS, HEAD_DIM), pos)
        k = rope(sw_k.reshape(B, L, SWA_KV_HEADS, HEAD_DIM), pos)
        o_sw = sliding_window_sink_attention(q, k, sw_v.reshape(B, L, SWA_KV_HEADS, HEAD_DIM),
                                             attn_sinks[l], valid)
        y_sb = (o_sb * jax.nn.silu(sb_z)) @ w_branch_sb[l]
        y_sw = (o_sw * jax.nn.silu(sw_z)) @ w_branch_swa[l]
        merged = jax.nn.sigmoid(g_sb) * y_sb + jax.nn.sigmoid(g_sw) * y_sw
        h = h + merged @ w_out[l]
    return rms_norm(h, final_norm_gain)[:, BLOCK:]
```

```python
import numpy as np
import concourse.bass as bass
import concourse.mybir as mybir
from concourse.bass_utils import run_bass_kernel_spmd
from contextlib import ExitStack

F32 = mybir.dt.float32
BF16 = mybir.dt.bfloat16
AF = mybir.ActivationFunctionType
ALU = mybir.AluOpType

D = 1024
KC = 8
N_META = 16
PAD = 112
EPS = 1e-6
NEG = -30000.0

C_KV, C_Q, C_SWK, C_SWV, C_SWQA, C_SWQB, C_ZSB, C_ZSW, C_GSB, C_GSW, WC = (
    0, 1024, 1536, 2048, 2176, 3200, 4224, 4736, 5760, 6784, 7808)


class _Inst:
    __slots__ = ("eng", "fn", "deps", "dma", "needed", "semkey", "val", "idx")

    def __init__(self, eng, fn, deps, dma):
        self.eng, self.fn, self.deps, self.dma = eng, fn, deps, dma
        self.needed = False
        self.semkey = None
        self.val = 0


class Prog:
    ENGS = ("pe", "act", "dve", "pool", "sp")

    def __init__(self):
        self.q = {e: [] for e in self.ENGS}
        self.res = {}
        self.order = []
        self.bar_from = 0
        self.frozen = False

    def add(self, eng, fn, reads=(), writes=(), dma=False, extra=()):
        if self.frozen:
            return None
        deps = []
        for r in reads:
            lw, _ = self.res.get(r, (None, None))
            if lw is not None:
                deps.append(lw)
        for w in writes:
            lw, rd = self.res.get(w, (None, []))
            if lw is not None:
                deps.append(lw)
            if rd:
                deps.extend(rd)
        deps.extend(extra)
        ins = _Inst(eng, fn, deps, dma)
        ins.idx = len(self.order)
        self.order.append(ins)
        self.q[eng].append(ins)
        for r in reads:
            lw, rd = self.res.get(r, (None, []))
            self.res[r] = (lw, rd + [ins])
        for w in writes:
            self.res[w] = (ins, [])
        return ins

    def group(self, eng, fns, reads=(), writes=()):
        n = len(fns)
        for i, fn in enumerate(fns):
            if i == 0 or i == n - 1:
                self.add(eng, fn, reads=reads, writes=writes)
            else:
                self.add(eng, fn)

    def barrier(self):
        if self.frozen:
            return
        lasts = [self.q[e][-1] for e in self.ENGS if self.q[e]]
        dmas = [i for i in self.order[self.bar_from:] if i.dma]
        self.bar_from = len(self.order)
        for e in self.ENGS:
            self.add(e, None, extra=lasts + dmas)


def _prepare(P, semctx, state):
    NSEM = 20
    for e in P.ENGS:
        k = 0
        last_on = {}
        for ins in P.q[e]:
            if ins.fn is not None and ins.dma:
                ins.semkey = ("dma", e, k % NSEM)
                k += 1
                prev = last_on.get(ins.semkey)
                if prev is not None:
                    ins.deps.append(prev)
                last_on[ins.semkey] = ins
    for ins in P.order:
        for d in ins.deps:
            if d is ins or d.fn is None:
                continue
            if (not d.dma) and d.eng == "pe" and ins.eng == "pe":
                continue
            d.needed = True
    counts = {}
    for e in P.ENGS:
        for ins in P.q[e]:
            if ins.fn is None:
                continue
            if ins.dma:
                pass
            elif ins.needed:
                ins.semkey = ("eng", e)
            else:
                continue
            counts[ins.semkey] = counts.get(ins.semkey, 0) + (16 if ins.dma else 1)
            ins.val = counts[ins.semkey]
    state["sems"] = {k: semctx("s_" + "_".join(str(x) for x in k)) for k in counts}
    state["prepared"] = True
    state["maxcount"] = max(counts.values()) if counts else 0


def _emit_engine(P, e, eng, state):
    sems = state["sems"]
    waited = {}
    for ins in P.q[e]:
        need = {}
        for d in ins.deps:
            if d.fn is None or d.semkey is None or d is ins:
                continue
            if (not d.dma) and d.eng == "pe" and e == "pe":
                continue
            if waited.get(d.semkey, 0) >= d.val:
                continue
            if need.get(d.semkey, 0) < d.val:
                need[d.semkey] = d.val
        for k, v in need.items():
            eng.wait_ge(sems[k], v)
            waited[k] = v
        if ins.fn is None:
            continue
        r = ins.fn(eng)
        if ins.semkey is not None:
            r.then_inc(sems[ins.semkey], 16 if ins.dma else 1)


def MM(out, lhsT, rhs, start, stop, skip=False):
    if skip:
        return lambda e: e.matmul(out, lhsT, rhs, start=start, stop=stop, skip_group_check=True)
    return lambda e: e.matmul(out, lhsT, rhs, start=start, stop=stop)


def TR(out, in_, ident):
    return lambda e: e.transpose(out, in_, ident)


def ACT(out, in_, func, bias=None, scale=None, accum_out=None):
    kw = {}
    if bias is not None:
        kw["bias"] = bias
    if scale is not None:
        kw["scale"] = scale
    if accum_out is not None:
        kw["accum_out"] = accum_out
    return lambda e: e.activation(out=out, in_=in_, func=func, **kw)


def TT(out, in0, in1, op):
    return lambda e: e.tensor_tensor(out=out, in0=in0, in1=in1, op=op)


def STT(out, in0, scalar, in1, op0, op1):
    return lambda e: e.scalar_tensor_tensor(out=out, in0=in0, scalar=scalar, in1=in1, op0=op0, op1=op1)


def TSMUL(out, in0, scalar):
    return lambda e: e.tensor_scalar(out=out, in0=in0, scalar1=scalar, scalar2=None, op0=ALU.mult)


def CP(out, in_):
    return lambda e: e.tensor_copy(out=out, in_=in_)


def RCP(out, in_):
    return lambda e: e.reciprocal(out=out, in_=in_)


def MSET(out, v):
    return lambda e: e.memset(out, v)


def DMA(out, in_):
    return lambda e: e.dma_start(out=out, in_=in_)


class _Stop(Exception):
    pass


def build(NSTEP, debug=False, stop_after=None):
    NB = 8 * NSTEP + 1
    L = 128 * NB
    NG = 2 * NSTEP
    NOWN = NSTEP * 640

    nc = bass.Bass("TRN2", target_bir_lowering=False)
    dt = nc.dram_tensor
    h = dt("h", [L, D], F32, kind="ExternalInput").ap()
    hown = dt("hown", [NOWN, D], F32, kind="ExternalInput").ap()
    wcat = dt("wcat", [D, WC], F32, kind="ExternalInput").ap()
    wb2 = dt("wb2", [2560, D], F32, kind="ExternalInput").ap()
    gains = dt("gains", [2, D], F32, kind="ExternalInput").ap()
    sinks = dt("sinks", [2, 8], F32, kind="ExternalInput").ap()
    tabk = dt("tabk", [2, 128, NOWN], F32, kind="ExternalInput").ap()
    masks_in = dt("masks", [128, 8 * 512 + 3 * 128], F32, kind="ExternalInput").ap()
    consts_in = dt("consts", [128, 5 * 128], F32, kind="ExternalInput").ap()
    padb_in = dt("padb", [128, 2], F32, kind="ExternalInput").ap()
    y = dt("y", [NSTEP * 512, D], F32, kind="ExternalOutput").ap()

    okind = "ExternalOutput" if debug else "Internal"
    KTs = dt("KTs", [4, 128, L], BF16, kind=okind).ap()
    Vs = dt("Vs", [4, 128, NB * 128], BF16, kind=okind).ap()
    wcat_bf = dt("wcat_bf", [D, WC], BF16, kind="Internal").ap()
    wb2_bf = dt("wb2_bf", [2560, D], BF16, kind="Internal").ap()

    P = Prog()
    es = ExitStack()
    with es:
        def sb(name, shape, dtype):
            return es.enter_context(nc.sbuf_tensor(name, shape, dtype))

        def ps(name, shape, dtype=F32):
            return es.enter_context(nc.psum_tensor(name, shape, dtype))

        ring = [sb("ring%d" % i, [128, 4096], BF16) for i in range(4)]
        kvr = [sb("kvr%d" % i, [128, 2048], BF16) for i in range(4)]
        xt = [sb("xt%d" % i, [128, D], F32) for i in range(3)]
        xnb = [sb("xnb%d" % i, [128, D], BF16) for i in range(2)]
        xnT = [sb("xnT%d" % i, [128, KC, 640], BF16) for i in range(2)]
        gain_bc = sb("gain_bc", [128, D], F32)
        fgain_bc = sb("fgain_bc", [128, D], F32)
        arena = sb("arena", [128, 8192], BF16)
        KTst = [arena[:, i * 2048:(i + 1) * 2048].rearrange("p (h n) -> p h n", h=4) for i in range(2)]
        Vst = [arena[:, 4096 + i * 2048: 4096 + (i + 1) * 2048].rearrange("p (j n) -> p j n", j=4) for i in range(2)]
        oT_sw = arena[:, :].bitcast(F32).rearrange("p (c n) -> p c n", c=8)
        ss = sb("ss", [128, 8], F32)
        rstd = sb("rstd", [128, 8], F32)
        qT2 = [sb("qT_%d" % i, [128, 4, 512], BF16) for i in range(2)]
        xnT_f = sb("xnT_f", [128, KC, 128], BF16)
        KTst_f = [sb("KTst_f%d" % i, [128, 4, 128], BF16) for i in range(2)]
        Vst_f = [sb("Vst_f%d" % i, [128, 512], BF16) for i in range(2)]
        arena2 = sb("arena2", [128, 20480], BF16)

        def a2(off_kb, shape, dtype):
            n = 1
            for d_ in shape[1:]:
                n *= d_
            nb16 = n * (2 if dtype == F32 else 1)
            v = arena2[:, off_kb * 512: off_kb * 512 + nb16]
            if dtype == F32:
                v = v.bitcast(F32)
            if len(shape) == 3:
                v = v.rearrange("p (a n) -> p a n", a=shape[1])
            return v
        Eb = a2(0, [128, 2, 512], F32)
        Lb = [a2(4 + 2 * i, [128, 2, 512], BF16) for i in range(2)]
        Wb = [a2(8 + 2 * i, [128, 2, 512], BF16) for i in range(2)]
        Rb = a2(12, [128, 2, 512], BF16)
        Pb = a2(14, [128, 2048], BF16)
        oT_sb = sb("oT_sb", [128, 4, 512], F32)
        masks = sb("masks_sb", [128, 8, 512], BF16)
        msw = sb("msw", [128, 3, 128], BF16)
        cst = sb("cst", [128, 5, 128], BF16)
        ident, negU, negOnes, zeros, ones = (cst[:, i, :] for i in range(5))
        padb = sb("padb_sb", [128, 2], F32)
        es_t = sb("es_t", [128, 8], F32)
        qT_sw = sb("qT_sw", [128, 8, 512], BF16)
        kt_sw = sb("kt_sw", [128, 2, 640], BF16)
        v_sw = sb("v_sw", [128, 5, 128], BF16)
        tk = sb("tk", [128, 2, 640], F32)
        rt1 = sb("rt1", [128, 640], F32)
        rt2 = sb("rt2", [128, 640], F32)
        sg = [a2(2 * i, [128, 512], F32) for i in range(4)]
        t1 = [a2(8 + 2 * i, [128, 512], F32) for i in range(2)]
        mT = a2(12, [128, 8, 512], BF16)
        outt = [a2(20 + 4 * i, [128, D], F32) for i in range(2)]
        gT_sb = a2(28, [128, 4, 512], BF16)
        gT_sw = a2(32, [128, 8, 512], BF16)
        junk = a2(36, [128, D], BF16)

        pz = [ps("pz%d" % i, [128, 1024]) for i in range(3)]
        po = [ps("po%d" % i, [128, 512]) for i in range(2)]
        banks = [pz[0][:, 0:512], pz[0][:, 512:1024], pz[1][:, 0:512], pz[1][:, 512:1024],
                 pz[2][:, 0:512], pz[2][:, 512:1024], po[0][:, :], po[1][:, :]]
        bname = ["B%d" % i for i in range(8)]

        def dma(eng, out, in_, reads=(), writes=()):
            return P.add(eng, DMA(out, in_), reads=reads, writes=writes, dma=True)

        def stage(name):
            if stop_after == name:
                P.barrier()
                P.frozen = True

        dma("pool", wcat_bf[:, C_KV:C_KV + 1024], wcat[:, C_KV:C_KV + 1024], writes=["wbf_0"])
        dma("pool", cst[:].rearrange("p a b -> p (a b)"), consts_in[:, :], writes=["cst"])
        dma("pool", masks[:].rearrange("p a b -> p (a b)"), masks_in[:, 0:4096], writes=["masks"])
        dma("pool", msw[:].rearrange("p a b -> p (a b)"), masks_in[:, 4096:4096 + 384], writes=["msw"])
        dma("sp", gain_bc[:], gains[0].partition_broadcast(128), writes=["gain"])
        dma("sp", fgain_bc[:], gains[1].partition_broadcast(128), writes=["fgain"])
        dma("sp", padb[:], padb_in[:, :], writes=["padb"])
        dma("sp", es_t[0:64, :], sinks[0].partition_broadcast(64), writes=["es_a"])
        dma("sp", es_t[64:128, :], sinks[1].partition_broadcast(64), writes=["es_b"])
        for c0 in range(1024, WC, 1024):
            c1 = min(WC, c0 + 1024)
            dma("pool", wcat_bf[:, c0:c1], wcat[:, c0:c1], writes=["wbf_%d" % c0])
        for r0 in range(0, 2560, 512):
            dma("pool", wb2_bf[r0:r0 + 512, :], wb2[r0:r0 + 512, :], writes=["wb2_%d" % r0])

        def wres_for(c0, c1):
            return sorted({"wbf_%d" % ((c // 1024) * 1024) for c in range(c0, c1, 128)})

        P.add("act", ACT(es_t[:], es_t[:], AF.Exp), reads=["es_a", "es_b"], writes=["es_t"])

        wv = wcat_bf.rearrange("(c p) n -> p c n", p=128)

        def load_unit(slot, c0, ncol=512):
            view = ring[slot][:, 0:8 * ncol].rearrange("p (c n) -> p c n", c=8)
            dma("sp", view, wv[:, :, c0:c0 + ncol], reads=wres_for(c0, c0 + ncol), writes=["ring%d" % slot])
            return view

        def load_unit_b2(slot, r0, nfc, c0):
            view = ring[slot][:, 0:nfc * 512].rearrange("p (c n) -> p c n", c=nfc)
            src = wb2_bf[r0:r0 + 128 * nfc, :].rearrange("(c p) n -> p c n", p=128)[:, :, c0:c0 + 512]
            rr = sorted({"wb2_%d" % ((r // 512) * 512) for r in range(r0, r0 + 128 * nfc, 128)})
            dma("sp", view, src, reads=rr, writes=["ring%d" % slot])
            return view

        stage("pro")
        xt_i = [0]

        def rms_scale(j, s):
            P.add("act", ACT(junk[:], xt[s][:], AF.Square, accum_out=ss[:, j:j + 1]),
                  reads=["xt%d" % s], writes=["junk", "ss%d" % j])
            P.add("act", ACT(rstd[:, j:j + 1], ss[:, j:j + 1], AF.Ln, scale=1.0 / D, bias=padb[:, 1:2]),
                  reads=["ss%d" % j, "padb"], writes=["rln%d" % j])
            P.add("act", ACT(rstd[:, j:j + 1], rstd[:, j:j + 1], AF.Exp, scale=-0.5),
                  reads=["rln%d" % j], writes=["rstd%d" % j])

        def norm_transpose(src_ap, nblk, xi):
            for j in range(nblk):
                s = xt_i[0] % 3
                xt_i[0] += 1
                dma("sp", xt[s][:], src_ap[j * 128:(j + 1) * 128, :], writes=["xt%d" % s])
                rms_scale(j, s)
                b = j % 2
                P.add("dve", STT(xnb[b][:], xt[s][:], rstd[:, j:j + 1], gain_bc[:], ALU.mult, ALU.mult),
                      reads=["xt%d" % s, "rstd%d" % j, "gain"], writes=["xnb%d" % b])
                bk = 6 + (j % 2)
                pb = banks[bk].bitcast(BF16)
                P.group("pe", [TR(pb[:, c * 128:(c + 1) * 128], xnb[b][:, c * 128:(c + 1) * 128], ident)
                               for c in range(KC)],
                        reads=["xnb%d" % b, "cst"], writes=[bname[bk]])
                P.add("act", ACT(xnT[xi][:, :, j * 128:(j + 1) * 128],
                                 pb[:, 0:1024].rearrange("p (c n) -> p c n", c=KC), AF.Copy),
                      reads=[bname[bk]], writes=["xnT%d_%d" % (xi, j)])

        def xres(xi, j0, j1):
            return ["xnT%d_%d" % (xi, j) for j in range(j0, j1)]

        evac_i = [0]

        def evac_copy(out_ap, in_ap, reads, writes, scale=None):
            k = evac_i[0]
            evac_i[0] += 1
            if k % 2 == 0:
                P.add("act", ACT(out_ap, in_ap, AF.Copy, scale=scale), reads=reads, writes=writes)
            elif scale is None:
                P.add("dve", CP(out_ap, in_ap), reads=reads, writes=writes)
            else:
                P.add("dve", TSMUL(out_ap, in_ap, scale), reads=reads, writes=writes)

        def proj_fm(w_view, col0, xT_ap, ntok, bk, reads):
            P.group("pe", [MM(banks[bk][:, 0:ntok], w_view[:, c, col0:col0 + 128], xT_ap[:, c, 0:ntok],
                              c == 0, c == KC - 1) for c in range(KC)],
                    reads=reads, writes=[bname[bk]])

        wkv0 = load_unit(0, C_KV)
        wkv1 = load_unit(1, C_KV + 512)
        groups = [(0, 1), (1, 4), (5, 4)] + ([(9, 4), (13, 4)] if NSTEP >= 2 else [])
        bki = [0]
        for gi, (b0, nblk) in enumerate(groups):
            ntok = nblk * 128
            xb = gi % 2
            st = gi % 2
            norm_transpose(h[b0 * 128:(b0 + nblk) * 128, :], nblk, xb)
            for hp in range(4):
                bk = bki[0] % 6
                bki[0] += 1
                proj_fm(wkv0, hp * 128, xnT[xb], ntok, bk, xres(xb, 0, nblk) + ["ring0"])
                evac_copy(KTst[st][:, hp, 0:ntok], banks[bk][:, 0:ntok], [bname[bk]], ["KTst%d_%d" % (st, hp)])
            for j in range(nblk):
                bk = bki[0] % 6
                bki[0] += 1
                P.group("pe", [MM(banks[bk][:, :], xnT[xb][:, c, j * 128:(j + 1) * 128], wkv1[:, c, :],
                                  c == 0, c == KC - 1) for c in range(KC)],
                        reads=xres(xb, j, j + 1) + ["ring1"], writes=[bname[bk]])
                evac_copy(Vst[st][:, j, :], banks[bk][:, :], [bname[bk]], ["Vst%d_%d" % (st, j)])
            t0 = b0 * 128
            dma("pool", KTs[:, :, t0:t0 + ntok].rearrange("h p t -> p h t"), KTst[st][:, :, 0:ntok],
                reads=["KTst%d_%d" % (st, hp) for hp in range(4)],
                writes=["KTs_b%d" % b for b in range(b0, b0 + nblk)])
            for hp in range(4):
                dma("pool", Vs[hp, :, t0:t0 + ntok].rearrange("p (j c) -> p j c", c=128),
                    Vst[st][:, 0:nblk, hp * 128:(hp + 1) * 128],
                    reads=["Vst%d_%d" % (st, j) for j in range(nblk)],
                    writes=["Vs_b%d_%d" % (b, hp) for b in range(b0, b0 + nblk)])

        stage("p1")

        def blk_group(kb):
            return 0 if kb == 0 else (kb - 1) // 4 + 1

        kv_i = [0]

        def fgroup(fns, reads, writes, split=True):
            if not split or len(fns) <= 4:
                P.group("pe", fns, reads=reads, writes=writes)
                return
            hlf = len(fns) // 2
            P.group("pe", fns[:hlf], reads=reads, writes=writes)
            yield
            P.group("pe", fns[hlf:], reads=reads, writes=writes)

        def kv_filler(m):
            dq = "pool"
            blocks = list(range(8 * m + 9, 8 * m + 17))
            w0 = ring[0][:, 0:4096].rearrange("p (c n) -> p c n", c=8)
            w1 = ring[1][:, 0:4096].rearrange("p (c n) -> p c n", c=8)
            dma(dq, w0, wv[:, :, C_KV:C_KV + 512], reads=wres_for(C_KV, C_KV + 512), writes=["ring0"])
            dma(dq, w1, wv[:, :, C_KV + 512:C_KV + 1024], reads=wres_for(C_KV + 512, C_KV + 1024), writes=["ring1"])
            slots = {}

            def xload(i):
                s = xt_i[0] % 3
                xt_i[0] += 1
                slots[i] = s
                b = blocks[i]
                dma(dq, xt[s][:], h[b * 128:(b + 1) * 128, :], writes=["xt%d" % s])

            xload(0)
            xload(1)
            yield
            bk = 7
            pb = banks[bk].bitcast(BF16)
            for i, b in enumerate(blocks):
                s = slots[i]
                st = i % 2
                rms_scale(5, s)
                P.add("dve", STT(xnb[st][:], xt[s][:], rstd[:, 5:6], gain_bc[:], ALU.mult, ALU.mult),
                      reads=["xt%d" % s, "rstd5", "gain"], writes=["xnb%d" % st])
                if i + 2 < len(blocks):
                    xload(i + 2)
                P.group("pe", [TR(pb[:, c * 128:(c + 1) * 128], xnb[st][:, c * 128:(c + 1) * 128], ident)
                               for c in range(KC)],
                        reads=["xnb%d" % st, "cst"], writes=[bname[bk]])
                P.add("dve", CP(xnT_f[:, :, :], pb[:, 0:1024].rearrange("p (c n) -> p c n", c=KC)),
                      reads=[bname[bk]], writes=["xnT_f"])
                yield
                for hh in range(2):
                    fns = []
                    for q_ in range(2):
                        hp = 2 * hh + q_
                        fns += [MM(banks[bk][:, q_ * 128:(q_ + 1) * 128], w0[:, c, hp * 128:(hp + 1) * 128],
                                   xnT_f[:, c, :], c == 0, c == KC - 1) for c in range(KC)]
                    yield from fgroup(fns, ["xnT_f", "ring0"], [bname[bk]])
                    P.add("dve", CP(KTst_f[st][:, 2 * hh:2 * hh + 2, :],
                                    banks[bk][:, 0:256].rearrange("p (a n) -> p a n", a=2)),
                          reads=[bname[bk]], writes=["KTst_f%d_%d" % (st, hh)])
                    yield
                yield from fgroup([MM(banks[bk][:, :], xnT_f[:, c, :], w1[:, c, :], c == 0, c == KC - 1)
                                   for c in range(KC)], ["xnT_f", "ring1"], [bname[bk]])
                P.add("dve", CP(Vst_f[st][:, :], banks[bk][:, :]), reads=[bname[bk]], writes=["Vst_f%d" % st])
                dma(dq, KTs[:, :, b * 128:(b + 1) * 128].rearrange("h p t -> p h t"), KTst_f[st][:, :, :],
                    reads=["KTst_f%d_0" % st, "KTst_f%d_1" % st], writes=["KTs_b%d" % b])
                for hp in range(4):
                    dma(dq, Vs[hp, :, b * 128:(b + 1) * 128], Vst_f[st][:, hp * 128:(hp + 1) * 128],
                        reads=["Vst_f%d" % st], writes=["Vs_b%d_%d" % (b, hp)])
                yield

        def front_end(m, banks_ok, dq, dve_only):
            par = m % 2
            own = hown[m * 640:(m + 1) * 640, :]
            xo = xnT[par]
            xq = xo[:, :, 128:640]
            xq_res = xres(par, 1, 5)
            xo_res = xres(par, 0, 5)
            bi = [0]

            def nb():
                b = banks_ok[bi[0] % len(banks_ok)]
                bi[0] += 1
                return b

            def ev(out_ap, in_ap, reads, writes, scale=None):
                if not dve_only:
                    evac_copy(out_ap, in_ap, reads, writes, scale)
                elif scale is None:
                    P.add("dve", CP(out_ap, in_ap), reads=reads, writes=writes)
                else:
                    P.add("dve", TSMUL(out_ap, in_ap, scale), reads=reads, writes=writes)

            def lu(slot, c0, ncol=512):
                view = ring[slot][:, 0:8 * ncol].rearrange("p (c n) -> p c n", c=8)
                dma(dq, view, wv[:, :, c0:c0 + ncol], reads=wres_for(c0, c0 + ncol), writes=["ring%d" % slot])
                return view

            slots = {}

            def xload(j):
                s = xt_i[0] % 3
                xt_i[0] += 1
                slots[j] = s
                dma(dq, xt[s][:], own[j * 128:(j + 1) * 128, :], writes=["xt%d" % s])

            xload(0)
            xload(1)
            wq = lu(0, C_Q)
            wk = lu(1, C_SWK)
            wvv = lu(2, C_SWV, ncol=128)
            wa0 = lu(3, C_SWQA)
            dma(dq, tk[:, 0, :], tabk[0, :, m * 640:(m + 1) * 640], writes=["tk0"])
            dma(dq, tk[:, 1, :], tabk[1, :, m * 640:(m + 1) * 640], writes=["tk1"])
            yield
            for j in range(5):
                s = slots[j]
                rms_scale(j, s)
                b = j % 2
                P.add("dve", STT(xnb[b][:], xt[s][:], rstd[:, j:j + 1], gain_bc[:], ALU.mult, ALU.mult),
                      reads=["xt%d" % s, "rstd%d" % j, "gain"], writes=["xnb%d" % b])
                if j + 2 < 5:
                    xload(j + 2)
                bk = nb()
                pb = banks[bk].bitcast(BF16)
                P.group("pe", [TR(pb[:, c * 128:(c + 1) * 128], xnb[b][:, c * 128:(c + 1) * 128], ident)
                               for c in range(KC)],
                        reads=["xnb%d" % b, "cst"], writes=[bname[bk]])
                ev(xo[:, :, j * 128:(j + 1) * 128], pb[:, 0:1024].rearrange("p (c n) -> p c n", c=KC),
                   [bname[bk]], ["xnT%d_%d" % (par, j)])
                yield
            for hp in range(4):
                bk = nb()
                yield from fgroup([MM(banks[bk][:, 0:512], wq[:, c, hp * 128:(hp + 1) * 128], xq[:, c, 0:512],
                                      c == 0, c == KC - 1) for c in range(KC)],
                                  xq_res + ["ring0"], [bname[bk]], dve_only)
                ev(qT2[par][:, hp, :], banks[bk][:, :], [bname[bk]], ["qT%d_%d" % (par, hp)], scale=0.125)
                yield
            wb0 = lu(0, C_SWQB)
            for g in range(2):
                for (n0, n1) in ((0, 512), (512, 640)):
                    for which, rt, tname in ((0, rt1, "tk0"), (1, rt2, "tk1")):
                        col = which * 256 + g * 128
                        bk = nb()
                        yield from fgroup([MM(banks[bk][:, 0:n1 - n0], wk[:, c, col:col + 128], xo[:, c, n0:n1],
                                              c == 0, c == KC - 1) for c in range(KC)],
                                          xo_res + ["ring1"], [bname[bk]], dve_only and n1 - n0 > 128)
                        P.add("dve", TT(rt[:, n0:n1], banks[bk][:, 0:n1 - n0], tk[:, which, n0:n1], ALU.mult),
                              reads=[bname[bk], tname], writes=["rt%d" % (which + 1)])
                        if which == 1:
                            P.add("dve", TT(kt_sw[:, g, n0:n1], rt1[:, n0:n1], rt2[:, n0:n1], ALU.add),
                                  reads=["rt1", "rt2"], writes=["kt_sw"])
                        yield
            wa1 = lu(1, C_SWQA + 512)
            bk = nb()
            for j in range(4):
                P.group("pe", [MM(banks[bk][:, j * 128:(j + 1) * 128], xo[:, c, j * 128:(j + 1) * 128], wvv[:, c, :],
                                  c == 0, c == KC - 1) for c in range(KC)],
                        reads=xo_res + ["ring2"], writes=[bname[bk]])
            ev(v_sw[:, 0:4, :], banks[bk][:, :].rearrange("p (j c) -> p j c", j=4), [bname[bk]], ["v_sw_a"])
            yield
            bk = nb()
            P.group("pe", [MM(banks[bk][:, 0:128], xo[:, c, 512:640], wvv[:, c, :], c == 0, c == KC - 1)
                           for c in range(KC)],
                    reads=xo_res + ["ring2"], writes=[bname[bk]])
            ev(v_sw[:, 4, :], banks[bk][:, 0:128], [bname[bk]], ["v_sw_b"])
            yield
            wb1 = lu(2, C_SWQB + 512)
            for half, (wa, ra, wb_, rb) in enumerate(((wa0, "ring3", wb0, "ring0"), (wa1, "ring1", wb1, "ring2"))):
                for cc in range(4):
                    cq = half * 4 + cc
                    for (wvw, rr, rt, which) in ((wa, ra, rt1, 0), (wb_, rb, rt2, 1)):
                        bk = nb()
                        yield from fgroup([MM(banks[bk][:, 0:512], wvw[:, c, cc * 128:(cc + 1) * 128],
                                              xq[:, c, 0:512], c == 0, c == KC - 1) for c in range(KC)],
                                          xq_res + [rr], [bname[bk]], dve_only)
                        P.add("dve", STT(rt[:, 0:512], banks[bk][:, :], 0.125, tk[:, which, 128:640],
                                         ALU.mult, ALU.mult),
                              reads=[bname[bk], "tk%d" % which], writes=["rt%d" % (which + 1)])
                        if which == 1:
                            P.add("dve", TT(qT_sw[:, cq, :], rt1[:, 0:512], rt2[:, 0:512], ALU.add),
                                  reads=["rt1", "rt2"], writes=["qT_sw%d" % cq])
                        yield

        for _ in front_end(0, list(range(8)), "sp", False):
            pass
        P.barrier()

        for m in range(NSTEP):
            par = m % 2
            xo = xnT[par]
            xq = xo[:, :, 128:640]
            xq_res = xres(par, 1, 5)
            own = hown[m * 640:(m + 1) * 640, :]
            qT = qT2[par]
            stage("proj")
            for g in range(2):
                for j in range(4):
                    for kb2 in range(2):
                        fns = []
                        for h8 in range(8):
                            cq = 4 * g + h8 // 2
                            hf = h8 % 2
                            zc = hf * 512 + (h8 // 2) * 128
                            fns.append(MM(pz[kb2][:, zc:zc + 128],
                                          kt_sw[hf * 64:(hf + 1) * 64, g, (j + kb2) * 128:(j + kb2 + 1) * 128],
                                          qT_sw[hf * 64:(hf + 1) * 64, cq, j * 128:(j + 1) * 128],
                                          h8 < 2, True, True))
                        mi = (2 if (m == 0 and j == 0) else 0) if kb2 == 0 else 1
                        for h8 in range(8):
                            fns.append(MM(pz[kb2][:, h8 * 128:(h8 + 1) * 128], ident, msw[:, mi, :], False, True, True))
                        P.group("pe", fns, reads=["kt_sw", "msw", "cst"] + ["qT_sw%d" % (4 * g + a) for a in range(4)],
                                writes=["B%d" % (2 * kb2), "B%d" % (2 * kb2 + 1)])
                        P.add("act", ACT(Pb[:, kb2 * 1024:(kb2 + 1) * 1024], pz[kb2][:, :], AF.Exp),
                              reads=["B%d" % (2 * kb2), "B%d" % (2 * kb2 + 1)], writes=["Pb%d" % kb2])
                    stage("swa1")
                    for which, bk in ((0, 4), (1, 5)):
                        fns = []
                        for hf in range(2):
                            for kb2 in range(2):
                                lhs = v_sw[:, j + kb2, g * 64:(g + 1) * 64] if which == 0 else ones[:, 0:64]
                                rhs = Pb[:, kb2 * 1024 + hf * 512:kb2 * 1024 + (hf + 1) * 512].rearrange(
                                    "p (a n) -> p a n", a=4)
                                fns.append(MM(banks[bk][hf * 64:(hf + 1) * 64, :].rearrange("p (a n) -> p a n", a=4),
                                              lhs, rhs, kb2 == 0, kb2 == 1))
                        P.group("pe", fns, reads=["Pb0", "Pb1", "v_sw_a", "v_sw_b", "cst"], writes=[bname[bk]])
                    stage("swa2")
                    P.add("dve", TT(rt1[:, 0:512].rearrange("p (a n) -> p a n", a=4),
                                    banks[5][:, :].rearrange("p (a n) -> p a n", a=4),
                                    es_t[:, 4 * g:4 * g + 4].unsqueeze(2).to_broadcast([128, 4, 128]), ALU.add),
                          reads=["B5", "es_t"], writes=["rt1"])
                    P.add("act", ACT(rt2[:, 0:512], rt1[:, 0:512], AF.Ln), reads=["rt1"], writes=["rt2"])
                    P.add("act", ACT(rt2[:, 0:512], rt2[:, 0:512], AF.Exp, scale=-1.0), reads=["rt2"], writes=["rt2"])
                    P.add("dve", TT(oT_sw[:, 4 * g:4 * g + 4, j * 128:(j + 1) * 128],
                                    banks[4][:, :].rearrange("p (a n) -> p a n", a=4),
                                    rt2[:, 0:512].rearrange("p (a n) -> p a n", a=4), ALU.mult),
                          reads=["B4", "rt2"], writes=["oT_sw"])

            stage("swa")
            its = []
            kmax = 8 * m + 8
            for hp in range(4):
                for kb in range(kmax, -1, -1):
                    rel = kb - 8 * m
                    its.append(dict(hp=hp, kb=kb, first=(kb == kmax), last=(kb == 0),
                                    c0=(128 * (rel - 5) if rel >= 5 else 0),
                                    mi=(rel - 1 if rel >= 1 else None)))
            loads = []
            for i, it in enumerate(its):
                ch = 0 if it["kb"] == 0 else (it["kb"] - 1) // 8 + 1
                it["kl"] = 0 if it["kb"] == 0 else (it["kb"] - 1) % 8
                if not loads or loads[-1][1:] != (it["hp"], ch):
                    loads.append((i, it["hp"], ch))
                it["ld"] = len(loads) - 1
            last_it = {}
            for i, it in enumerate(its):
                last_it[it["ld"]] = i
            slot_of = {}
            nl = [0]

            def issue_loads(t_done):
                while nl[0] < len(loads) and (nl[0] < 4 or last_it[nl[0] - 4] <= t_done):
                    _, hp, ch = loads[nl[0]]
                    slot = kv_i[0] % 4
                    kv_i[0] += 1
                    slot_of[nl[0]] = slot
                    tb0, nblk = (0, 1) if ch == 0 else (8 * (ch - 1) + 1, 8)
                    gr = ["KTs_b%d" % b for b in range(tb0, tb0 + nblk)]
                    gv = ["Vs_b%d_%d" % (b, hp) for b in range(tb0, tb0 + nblk)]
                    dma("sp", kvr[slot][:, 0:nblk * 128], KTs[hp, :, tb0 * 128:(tb0 + nblk) * 128],
                        reads=gr, writes=["kvK%d" % slot])
                    dma("sp", kvr[slot][:, 1024:1024 + nblk * 128], Vs[hp, :, tb0 * 128:(tb0 + nblk) * 128],
                        reads=gv, writes=["kvV%d" % slot])
                    nl[0] += 1

            def zres(t):
                k = t % 3
                return ["B%d" % (2 * k), "B%d" % (2 * k + 1)]

            def zv(t, c0):
                return pz[t % 3][:, :].rearrange("p (e n) -> p e n", e=2)[:, :, c0:512]

            def st_QK(t):
                it = its[t]
                slot = slot_of[it["ld"]]
                c0, kl, hp = it["c0"], it["kl"], it["hp"]
                fns = [MM(pz[t % 3][:, e_ * 512 + c0:(e_ + 1) * 512],
                          kvr[slot][e_ * 64:(e_ + 1) * 64, kl * 128:(kl + 1) * 128],
                          qT[e_ * 64:(e_ + 1) * 64, hp, c0:512], True, True, True) for e_ in range(2)]
                if it["mi"] is not None:
                    fns += [MM(pz[t % 3][:, e_ * 512 + c0:(e_ + 1) * 512], ident, masks[:, it["mi"], c0:512],
                               False, True, True) for e_ in range(2)]
                P.group("pe", fns, reads=["kvK%d" % slot, "qT%d_%d" % (par, hp), "masks", "cst"],
                        writes=zres(t))

            def st_E(t):
                it = its[t]
                c0 = it["c0"]
                bias = padb[:, 0:1] if it["kb"] == 0 else None
                P.add("act", ACT(Eb[:, :, c0:512], zv(t, c0), AF.Exp, bias=bias),
                      reads=zres(t) + ["padb"], writes=["Eb"])

            def st_L(t):
                c0 = its[t]["c0"]
                P.add("act", ACT(Lb[t % 2][:, :, c0:512], Eb[:, :, c0:512], AF.Ln, bias=1.0),
                      reads=["Eb"], writes=["Lb%d" % (t % 2)])

            def st_ADD(t):
                it = its[t]
                c0 = it["c0"]
                if it["first"]:
                    P.add("dve", MSET(Rb[:], 0.0), writes=["Rb"])
                if it["last"]:
                    return
                P.add("dve", TT(Rb[:, :, c0:512], Rb[:, :, c0:512], Lb[t % 2][:, :, c0:512], ALU.add),
                      reads=["Lb%d" % (t % 2), "Rb"], writes=["Rb"])

            def st_CS(t):
                it = its[t]
                c0 = it["c0"]
                fns = []
                for e_ in range(2):
                    o = pz[t % 3][:, e_ * 512 + c0:(e_ + 1) * 512]
                    fns.append(MM(o, negU, Lb[t % 2][:, e_, c0:512], False, True, True))
                    if not it["first"]:
                        fns.append(MM(o, negOnes, Rb[:, e_, c0:512], False, True, True))
                P.group("pe", fns, reads=["Lb%d" % (t % 2), "cst"] + ([] if it["first"] else ["Rb"]),
                        writes=zres(t))

            def st_W(t):
                it = its[t]
                c0 = it["c0"]
                bias = padb[:, 0:1] if it["kb"] == 0 else None
                P.add("act", ACT(Wb[t % 2][:, :, c0:512], zv(t, c0), AF.Exp, bias=bias),
                      reads=zres(t) + ["padb"], writes=["Wb%d" % (t % 2)])

            def st_PV(t):
                it = its[t]
                c0, kl, hp = it["c0"], it["kl"], it["hp"]
                slot = slot_of[it["ld"]]
                ob = 0
                fns = []
                for e_ in range(2):
                    vcol = 1024 + kl * 128 + e_ * 64
                    fns.append(MM(po[ob][e_ * 64:(e_ + 1) * 64, c0:512], kvr[slot][:, vcol:vcol + 64],
                                  Wb[t % 2][:, e_, c0:512], it["first"], True, True))
                P.group("pe", fns, reads=["Wb%d" % (t % 2), "kvV%d" % slot, "masks", "cst"], writes=["B6"])
                if it["last"]:
                    P.add("dve", CP(oT_sb[:, hp, :], po[ob][:, :]), reads=["B6"], writes=["oT_sb%d" % hp])

            n = len(its)
            issue_loads(-1)
            def both(m_):
                if m_ >= 1:
                    yield from kv_filler(m_)
                yield from front_end(m_ + 1, [7], "pool", True)
            filler = both(m) if m + 1 < NSTEP else None
            n_items = 140 if m >= 1 else 72
            stride = max(1, n // n_items)
            per_tick = 1
            pre = {}

            def prefetch_backend():
                pre["wz"] = load_unit(0, C_ZSB)
                pre["wzw"] = [load_unit(1, C_ZSW), load_unit(2, C_ZSW + 512)]
                pre["wbs0"] = load_unit_b2(3, 0, 4, 0)
            if filler is None:
                prefetch_backend()
            for t in range(-3, n):
                if 0 <= t + 2 < n:
                    st_E(t + 2)
                if 0 <= t + 1 < n:
                    st_CS(t + 1)
                    st_ADD(t + 1)
                if 0 <= t < n:
                    st_W(t)
                    st_PV(t)
                    issue_loads(t)
                if 0 <= t + 3 < n:
                    st_QK(t + 3)
                if 0 <= t + 2 < n:
                    st_L(t + 2)
                if filler is not None and t >= 0 and t % stride == 0:
                    try:
                        for _ in range(per_tick):
                            next(filler)
                    except StopIteration:
                        filler = None
                        prefetch_backend()
            if filler is not None:
                for _ in filler:
                    pass
                prefetch_backend()

            P.barrier()
            stage("sb")
            wz = pre["wz"]
            wzw = pre["wzw"]
            for ci in range(12):
                if ci < 4:
                    wvw, rr, col = wz, "ring0", ci * 128
                    osrc, odst, ores, dres = oT_sb[:, ci, :], gT_sb[:, ci, :], "oT_sb%d" % ci, "gT_sb%d" % ci
                else:
                    cq = ci - 4
                    wvw, rr, col = wzw[cq // 4], "ring%d" % (1 + cq // 4), (cq % 4) * 128
                    osrc, odst, ores, dres = oT_sw[:, cq, :], gT_sw[:, cq, :], "oT_sw", "gT_sw%d" % cq
                bk = ci % 4
                proj_fm(wvw, col, xq, 512, bk, xq_res + [rr])
                P.add("act", ACT(sg[bk][:], banks[bk][:, :], AF.Sigmoid), reads=[bname[bk]], writes=["sg%d" % bk])
                P.add("dve", TT(sg[bk][:], sg[bk][:], banks[bk][:, :], ALU.mult),
                      reads=[bname[bk], "sg%d" % bk], writes=["sg%d" % bk])
                P.add("dve", TT(odst, sg[bk][:], osrc, ALU.mult), reads=["sg%d" % bk, ores], writes=[dres])

            stage("gate")
            for hh in range(2):
                wbs = pre["wbs0"] if hh == 0 else load_unit_b2(3, 0, 4, hh * 512)
                wbw = load_unit_b2(0, 512, 8, hh * 512)
                wgs = load_unit(1, C_GSB + hh * 512)
                wgw = load_unit(2, C_GSW + hh * 512)
                for cc in range(4):
                    c_out = hh * 4 + cc
                    q4 = (cc % 2) * 4
                    b_ys, b_yw, b_gs, b_gw = q4, q4 + 1, q4 + 2, q4 + 3
                    P.group("pe", [MM(banks[b_ys][:, :], wbs[:, fc, cc * 128:(cc + 1) * 128], gT_sb[:, fc, :],
                                      fc == 0, fc == 3) for fc in range(4)],
                            reads=["ring3"] + ["gT_sb%d" % k for k in range(4)], writes=[bname[b_ys]])
                    P.group("pe", [MM(banks[b_yw][:, :], wbw[:, fc, cc * 128:(cc + 1) * 128], gT_sw[:, fc, :],
                                      fc == 0, fc == 7) for fc in range(8)],
                            reads=["ring0"] + ["gT_sw%d" % k for k in range(8)], writes=[bname[b_yw]])
                    proj_fm(wgs, cc * 128, xq, 512, b_gs, xq_res + ["ring1"])
                    proj_fm(wgw, cc * 128, xq, 512, b_gw, xq_res + ["ring2"])
                    ia, ib = (cc % 2) * 2, (cc % 2) * 2 + 1
                    sA, sB, rA, rB = sg[ia], sg[ib], "sg%d" % ia, "sg%d" % ib
                    tt, rtt = t1[cc % 2], "t1_%d" % (cc % 2)
                    P.add("act", ACT(sA[:], banks[b_gs][:, :], AF.Sigmoid), reads=[bname[b_gs]], writes=[rA])
                    P.add("act", ACT(sB[:], banks[b_gw][:, :], AF.Sigmoid), reads=[bname[b_gw]], writes=[rB])
                    P.add("dve", TT(tt[:], sA[:], banks[b_ys][:, :], ALU.mult), reads=[rA, bname[b_ys]], writes=[rtt])
                    P.add("dve", TT(sB[:], sB[:], banks[b_yw][:, :], ALU.mult), reads=[rB, bname[b_yw]], writes=[rB])
                    P.add("dve", TT(mT[:, c_out, :], tt[:], sB[:], ALU.add), reads=[rB, rtt], writes=["mT%d" % c_out])

            stage("merge")
            wo = [load_unit_b2(3, 1536, 8, 0), load_unit_b2(0, 1536, 8, 512)]
            wor = ["ring3", "ring0"]
            for j in range(4):
                s = xt_i[0] % 3
                xt_i[0] += 1
                dma("sp", xt[s][:], own[128 + j * 128:128 + (j + 1) * 128, :], writes=["xt%d" % s])
                for hh in range(2):
                    bk = (j % 2) * 2 + hh
                    P.group("pe", [MM(banks[bk][:, :], mT[:, c, j * 128:(j + 1) * 128], wo[hh][:, c, :],
                                      c == 0, c == KC - 1) for c in range(KC)],
                            reads=[wor[hh]] + ["mT%d" % k for k in range(8)], writes=[bname[bk]])
                    P.add("dve", TT(xt[s][:, hh * 512:(hh + 1) * 512], xt[s][:, hh * 512:(hh + 1) * 512],
                                    banks[bk][:, :], ALU.add),
                          reads=[bname[bk], "xt%d" % s], writes=["xt%d" % s])
                rms_scale(j, s)
                ob = j % 2
                P.add("dve", STT(outt[ob][:], xt[s][:], rstd[:, j:j + 1], fgain_bc[:], ALU.mult, ALU.mult),
                      reads=["xt%d" % s, "rstd%d" % j, "fgain"], writes=["outt%d" % ob])
                r0 = m * 512 + j * 128
                dma("pool", y[r0:r0 + 128, :], outt[ob][:], reads=["outt%d" % ob], writes=["y_%d_%d" % (m, j)])
            P.barrier()

        sem_stack = ExitStack()
        with sem_stack:
            def semctx(name):
                return sem_stack.enter_context(nc.semaphore(name))
            state = {}
            _prepare(P, semctx, state)
            with nc.Block() as block:
                block.tensor(lambda eng: _emit_engine(P, "pe", eng, state))
                block.scalar(lambda eng: _emit_engine(P, "act", eng, state))
                block.vector(lambda eng: _emit_engine(P, "dve", eng, state))
                block.gpsimd(lambda eng: _emit_engine(P, "pool", eng, state))
                block.sync(lambda eng: _emit_engine(P, "sp", eng, state))
    build.last_stats = dict(n_inst={e: len(P.q[e]) for e in P.ENGS}, maxcount=state["maxcount"])
    return nc


def _host_prep(x, meta_tokens, norm_gain, w_in, w_branch_sb, w_branch_swa, w_out, attn_sinks, final_norm_gain):
    B, SEQ, _ = x.shape
    NSTEP = SEQ // 1024
    NB = 8 * NSTEP + 1
    f32 = np.float32
    w = np.asarray(w_in[0], f32)
    offs = np.cumsum([0, 512, 512, 512, 1024, 128, 128, 512, 1024, 1024, 1024])
    sb_q, sb_k, sb_v, sw_q, sw_k, sw_v, sb_z, sw_z, g_sb, g_sw = [w[:, offs[i]:offs[i + 1]] for i in range(10)]
    perm64 = np.concatenate([np.arange(32, 64), np.arange(0, 32)])
    kA = np.concatenate([sw_k[:, 0:64], sw_k[:, 0:64], sw_k[:, 64:128], sw_k[:, 64:128]], axis=1)
    kB0 = sw_k[:, 0:64][:, perm64]
    kB1 = sw_k[:, 64:128][:, perm64]
    kB = np.concatenate([kB0, kB0, kB1, kB1], axis=1)
    qperm = np.concatenate([hh * 64 + perm64 for hh in range(16)])
    wcat = np.concatenate([sb_k, sb_v, sb_q, kA, kB, sw_v, sw_q, sw_q[:, qperm], sb_z, sw_z, g_sb, g_sw], axis=1)
    assert wcat.shape[1] == WC
    wcat = np.ascontiguousarray(wcat, f32)
    wb2 = np.ascontiguousarray(np.concatenate([w_branch_sb[0], w_branch_swa[0], w_out[0]], axis=0), f32)
    gains = np.ascontiguousarray(np.stack([norm_gain[0], final_norm_gain]), f32)
    sk = np.asarray(attn_sinks[0], f32)
    sinks = np.ascontiguousarray(np.stack([sk[0::2], sk[1::2]]), f32)
    half = 32
    inv = 10000.0 ** (-np.arange(half, dtype=np.float64) / half)
    dd = np.arange(128) % 64
    fi = dd % 32
    sign = np.where(dd < 32, -1.0, 1.0).astype(f32)

    def tables(pos):
        ang = pos.astype(np.float64)[None, :] * inv[fi][:, None]
        return np.cos(ang).astype(f32), (np.sin(ang) * sign[:, None]).astype(f32)

    consts = np.zeros((128, 5, 128), f32)
    consts[:, 0, :] = np.eye(128, dtype=f32)
    jj, s_ = np.meshgrid(np.arange(128), np.arange(128), indexing="ij")
    consts[:, 1, :] = np.where(jj >= s_, -1.0, 0.0)
    consts[:, 2, :] = -1.0
    consts[:, 3, :] = 0.0
    consts[:, 4, :] = 1.0
    consts = consts.reshape(128, 640)
    padb = np.zeros((128, 2), f32)
    padb[:PAD, 0] = NEG
    padb[:, 1] = EPS
    in_maps = []
    for b in range(B):
        hb = np.concatenate([np.zeros((PAD, D), f32), np.asarray(meta_tokens, f32), np.asarray(x[b], f32)], axis=0)
        for r in range(2):
            rows = []
            posk = []
            posq = []
            for m in range(NSTEP):
                first = 8 * m + 1 + 4 * r
                rows.append(hb[(first - 1) * 128:(first + 4) * 128])
                p = np.arange((first - 1) * 128, (first + 4) * 128) - PAD
                posk.append(p)
                posq.append(p[128:])
            hown = np.ascontiguousarray(np.concatenate(rows, axis=0))
            ck, sk_ = tables(np.concatenate(posk))
            tabk = np.ascontiguousarray(np.stack([ck, sk_]))
            mk = np.zeros((128, 8, 512), f32)
            s_l = np.arange(128)[:, None]
            t_l = np.arange(128)[None, :]
            tri = np.where(s_l < t_l, 0.0, NEG).astype(f32)
            for rel in range(1, 9):
                for qi in range(4):
                    qrel = 1 + 4 * r + qi
                    if rel < qrel:
                        blkm = 0.0
                    elif rel == qrel:
                        blkm = tri
                    else:
                        blkm = NEG
                    mk[:, rel - 1, qi * 128:(qi + 1) * 128] = blkm
            msw = np.zeros((128, 3, 128), f32)
            msw[:, 0, :] = np.where(s_l > t_l, 0.0, NEG)
            msw[:, 1, :] = np.where(s_l <= t_l, 0.0, NEG)
            msw[:, 2, :] = msw[:, 0, :]
            if r == 0:
                msw[:PAD, 2, :] = NEG
            masks = np.ascontiguousarray(np.concatenate([mk.reshape(128, 4096), msw.reshape(128, 384)], axis=1))
            in_maps.append(dict(h=np.ascontiguousarray(hb), hown=hown, wcat=wcat, wb2=wb2, gains=gains, sinks=sinks,
                                tabk=tabk, masks=masks, consts=consts, padb=padb))
    return NSTEP, in_maps


_NC_CACHE = {}


def run(x, meta_tokens, norm_gain, w_in, w_branch_sb, w_branch_swa, w_out, attn_sinks, final_norm_gain, debug=False,
        trace=False, stop_after=None):
    x = np.asarray(x)
    B, SEQ, _ = x.shape
    NSTEP, in_maps = _host_prep(x, meta_tokens, norm_gain, w_in, w_branch_sb, w_branch_swa, w_out, attn_sinks,
                                final_norm_gain)
    key = (NSTEP, debug, stop_after)
    if key not in _NC_CACHE:
        _NC_CACHE[key] = build(NSTEP, debug, stop_after)
    nc = _NC_CACHE[key]
    res = run_bass_kernel_spmd(nc, in_maps, core_ids=list(range(2 * B)), trace=trace)
    out = np.zeros((B, SEQ, D), np.float32)
    for b in range(B):
        for r in range(2):
            yy = res.results[2 * b + r]["y"]
            for m in range(NSTEP):
                first = 8 * m + 1 + 4 * r
                out[b, (first - 1) * 128:(first + 3) * 128] = yy[m * 512:(m + 1) * 512]
    if debug:
        return out, res
    return out


def kernel(x, meta_tokens, norm_gain, w_in, w_branch_sb, w_branch_swa, w_out, attn_sinks, final_norm_gain):
    return run(np.asarray(x), np.asarray(meta_tokens), np.asarray(norm_gain), np.asarray(w_in),
               np.asarray(w_branch_sb), np.asarray(w_branch_swa), np.asarray(w_out), np.asarray(attn_sinks),
               np.asarray(final_norm_gain))
```

```python
import numpy as np
import concourse.bass as bass
import concourse.mybir as mybir
from concourse.bass_utils import run_bass_kernel_spmd
from contextlib import ExitStack

F32 = mybir.dt.float32
BF16 = mybir.dt.bfloat16
AF = mybir.ActivationFunctionType
ALU = mybir.AluOpType

D = 1024
KC = 8
N_META = 16
PAD = 112
EPS = 1e-6
NEG = -30000.0

C_KV, C_Q, C_SWK, C_SWV, C_SWQA, C_SWQB, C_ZSB, C_ZSW, C_GSB, C_GSW, WC = (
    0, 1024, 1536, 2048, 2176, 3200, 4224, 4736, 5760, 6784, 7808)


class _Inst:
    __slots__ = ("eng", "fn", "deps", "dma", "needed", "semkey", "val", "idx")

    def __init__(self, eng, fn, deps, dma):
        self.eng, self.fn, self.deps, self.dma = eng, fn, deps, dma
        self.needed = False
        self.semkey = None
        self.val = 0


class Prog:
    ENGS = ("pe", "act", "dve", "pool", "sp")

    def __init__(self):
        self.q = {e: [] for e in self.ENGS}
        self.res = {}
        self.order = []
        self.bar_from = 0
        self.frozen = False

    def add(self, eng, fn, reads=(), writes=(), dma=False, extra=()):
        if self.frozen:
            return None
        deps = []
        for r in reads:
            lw, _ = self.res.get(r, (None, None))
            if lw is not None:
                deps.append(lw)
        for w in writes:
            lw, rd = self.res.get(w, (None, []))
            if lw is not None:
                deps.append(lw)
            if rd:
                deps.extend(rd)
        deps.extend(extra)
        ins = _Inst(eng, fn, deps, dma)
        ins.idx = len(self.order)
        self.order.append(ins)
        self.q[eng].append(ins)
        for r in reads:
            lw, rd = self.res.get(r, (None, []))
            self.res[r] = (lw, rd + [ins])
        for w in writes:
            self.res[w] = (ins, [])
        return ins

    def group(self, eng, fns, reads=(), writes=()):
        n = len(fns)
        for i, fn in enumerate(fns):
            if i == 0 or i == n - 1:
                self.add(eng, fn, reads=reads, writes=writes)
            else:
                self.add(eng, fn)

    def barrier(self):
        if self.frozen:
            return
        lasts = [self.q[e][-1] for e in self.ENGS if self.q[e]]
        dmas = [i for i in self.order[self.bar_from:] if i.dma]
        self.bar_from = len(self.order)
        for e in self.ENGS:
            self.add(e, None, extra=lasts + dmas)


def _prepare(P, semctx, state):
    NSEM = 20
    for e in P.ENGS:
        k = 0
        last_on = {}
        for ins in P.q[e]:
            if ins.fn is not None and ins.dma:
                ins.semkey = ("dma", e, k % NSEM)
                k += 1
                prev = last_on.get(ins.semkey)
                if prev is not None:
                    ins.deps.append(prev)
                last_on[ins.semkey] = ins
    for ins in P.order:
        for d in ins.deps:
            if d is ins or d.fn is None:
                continue
            if (not d.dma) and d.eng == "pe" and ins.eng == "pe":
                continue
            d.needed = True
    counts = {}
    for e in P.ENGS:
        for ins in P.q[e]:
            if ins.fn is None:
                continue
            if ins.dma:
                pass
            elif ins.needed:
                ins.semkey = ("eng", e)
            else:
                continue
            counts[ins.semkey] = counts.get(ins.semkey, 0) + (16 if ins.dma else 1)
            ins.val = counts[ins.semkey]
    state["sems"] = {k: semctx("s_" + "_".join(str(x) for x in k)) for k in counts}
    state["prepared"] = True
    state["maxcount"] = max(counts.values()) if counts else 0


def _emit_engine(P, e, eng, state):
    sems = state["sems"]
    waited = {}
    for ins in P.q[e]:
        need = {}
        for d in ins.deps:
            if d.fn is None or d.semkey is None or d is ins:
                continue
            if (not d.dma) and d.eng == "pe" and e == "pe":
                continue
            if waited.get(d.semkey, 0) >= d.val:
                continue
            if need.get(d.semkey, 0) < d.val:
                need[d.semkey] = d.val
        for k, v in need.items():
            eng.wait_ge(sems[k], v)
            waited[k] = v
        if ins.fn is None:
            continue
        r = ins.fn(eng)
        if ins.semkey is not None:
            r.then_inc(sems[ins.semkey], 16 if ins.dma else 1)


def MM(out, lhsT, rhs, start, stop, skip=False):
    if skip:
        return lambda e: e.matmul(out, lhsT, rhs, start=start, stop=stop, skip_group_check=True)
    return lambda e: e.matmul(out, lhsT, rhs, start=start, stop=stop)


def TR(out, in_, ident):
    return lambda e: e.transpose(out, in_, ident)


def ACT(out, in_, func, bias=None, scale=None, accum_out=None):
    kw = {}
    if bias is not None:
        kw["bias"] = bias
    if scale is not None:
        kw["scale"] = scale
    if accum_out is not None:
        kw["accum_out"] = accum_out
    return lambda e: e.activation(out=out, in_=in_, func=func, **kw)


def TT(out, in0, in1, op):
    return lambda e: e.tensor_tensor(out=out, in0=in0, in1=in1, op=op)


def STT(out, in0, scalar, in1, op0, op1):
    return lambda e: e.scalar_tensor_tensor(out=out, in0=in0, scalar=scalar, in1=in1, op0=op0, op1=op1)


def TSMUL(out, in0, scalar):
    return lambda e: e.tensor_scalar(out=out, in0=in0, scalar1=scalar, scalar2=None, op0=ALU.mult)


def CP(out, in_):
    return lambda e: e.tensor_copy(out=out, in_=in_)


def RCP(out, in_):
    return lambda e: e.reciprocal(out=out, in_=in_)


def MSET(out, v):
    return lambda e: e.memset(out, v)


def DMA(out, in_):
    return lambda e: e.dma_start(out=out, in_=in_)


class _Stop(Exception):
    pass


def build(NSTEP, debug=False, stop_after=None):
    NB = 8 * NSTEP + 1
    L = 128 * NB
    NG = 2 * NSTEP
    NOWN = NSTEP * 640

    nc = bass.Bass("TRN2", target_bir_lowering=False)
    dt = nc.dram_tensor
    h = dt("h", [L, D], F32, kind="ExternalInput").ap()
    hown = dt("hown", [NOWN, D], F32, kind="ExternalInput").ap()
    wcat = dt("wcat", [D, WC], F32, kind="ExternalInput").ap()
    wb2 = dt("wb2", [2560, D], F32, kind="ExternalInput").ap()
    gains = dt("gains", [2, D], F32, kind="ExternalInput").ap()
    sinks = dt("sinks", [2, 8], F32, kind="ExternalInput").ap()
    tabk = dt("tabk", [2, 128, NOWN], F32, kind="ExternalInput").ap()
    masks_in = dt("masks", [128, 8 * 512 + 3 * 128], F32, kind="ExternalInput").ap()
    consts_in = dt("consts", [128, 5 * 128], F32, kind="ExternalInput").ap()
    padb_in = dt("padb", [128, 2], F32, kind="ExternalInput").ap()
    y = dt("y", [NSTEP * 512, D], F32, kind="ExternalOutput").ap()

    okind = "ExternalOutput" if debug else "Internal"
    KTs = dt("KTs", [4, 128, L], BF16, kind=okind).ap()
    Vs = dt("Vs", [4, 128, NB * 128], BF16, kind=okind).ap()
    wcat_bf = dt("wcat_bf", [D, WC], BF16, kind="Internal").ap()
    wb2_bf = dt("wb2_bf", [2560, D], BF16, kind="Internal").ap()

    P = Prog()
    es = ExitStack()
    with es:
        def sb(name, shape, dtype):
            return es.enter_context(nc.sbuf_tensor(name, shape, dtype))

        def ps(name, shape, dtype=F32):
            return es.enter_context(nc.psum_tensor(name, shape, dtype))

        ring = [sb("ring%d" % i, [128, 4096], BF16) for i in range(4)]
        kvr = [sb("kvr%d" % i, [128, 2048], BF16) for i in range(4)]
        xt = [sb("xt%d" % i, [128, D], F32) for i in range(3)]
        xnb = [sb("xnb%d" % i, [128, D], BF16) for i in range(2)]
        xnT = [sb("xnT%d" % i, [128, KC, 640], BF16) for i in range(2)]
        gain_bc = sb("gain_bc", [128, D], F32)
        fgain_bc = sb("fgain_bc", [128, D], F32)
        arena = sb("arena", [128, 8192], BF16)
        KTst = [arena[:, i * 2048:(i + 1) * 2048].rearrange("p (h n) -> p h n", h=4) for i in range(2)]
        Vst = [arena[:, 4096 + i * 2048: 4096 + (i + 1) * 2048].rearrange("p (j n) -> p j n", j=4) for i in range(2)]
        oT_sw = arena[:, :].bitcast(F32).rearrange("p (c n) -> p c n", c=8)
        ss = sb("ss", [128, 8], F32)
        rstd = sb("rstd", [128, 8], F32)
        qT2 = [sb("qT_%d" % i, [128, 4, 512], BF16) for i in range(2)]
        xnT_f = sb("xnT_f", [128, KC, 128], BF16)
        KTst_f = [sb("KTst_f%d" % i, [128, 4, 128], BF16) for i in range(2)]
        Vst_f = [sb("Vst_f%d" % i, [128, 512], BF16) for i in range(2)]
        arena2 = sb("arena2", [128, 20480], BF16)

        def a2(off_kb, shape, dtype):
            n = 1
            for d_ in shape[1:]:
                n *= d_
            nb16 = n * (2 if dtype == F32 else 1)
            v = arena2[:, off_kb * 512: off_kb * 512 + nb16]
            if dtype == F32:
                v = v.bitcast(F32)
            if len(shape) == 3:
                v = v.rearrange("p (a n) -> p a n", a=shape[1])
            return v
        Eb = a2(0, [128, 2, 512], F32)
        Lb = [a2(4 + 2 * i, [128, 2, 512], BF16) for i in range(2)]
        Wb = [a2(8 + 2 * i, [128, 2, 512], BF16) for i in range(2)]
        Rb = a2(12, [128, 2, 512], BF16)
        Pb = a2(14, [128, 2048], BF16)
        oT_sb = sb("oT_sb", [128, 4, 512], F32)
        masks = sb("masks_sb", [128, 8, 512], BF16)
        msw = sb("msw", [128, 3, 128], BF16)
        cst = sb("cst", [128, 5, 128], BF16)
        ident, negU, negOnes, zeros, ones = (cst[:, i, :] for i in range(5))
        padb = sb("padb_sb", [128, 2], F32)
        es_t = sb("es_t", [128, 8], F32)
        qT_sw = sb("qT_sw", [128, 8, 512], BF16)
        kt_sw = sb("kt_sw", [128, 2, 640], BF16)
        v_sw = sb("v_sw", [128, 5, 128], BF16)
        tk = sb("tk", [128, 2, 640], F32)
        rt1 = sb("rt1", [128, 640], F32)
        rt2 = sb("rt2", [128, 640], F32)
        sg = [a2(2 * i, [128, 512], F32) for i in range(4)]
        t1 = [a2(8 + 2 * i, [128, 512], F32) for i in range(2)]
        mT = a2(12, [128, 8, 512], BF16)
        outt = [a2(20 + 4 * i, [128, D], F32) for i in range(2)]
        gT_sb = a2(28, [128, 4, 512], BF16)
        gT_sw = a2(32, [128, 8, 512], BF16)
        junk = a2(36, [128, D], BF16)

        pz = [ps("pz%d" % i, [128, 1024]) for i in range(3)]
        po = [ps("po%d" % i, [128, 512]) for i in range(2)]
        banks = [pz[0][:, 0:512], pz[0][:, 512:1024], pz[1][:, 0:512], pz[1][:, 512:1024],
                 pz[2][:, 0:512], pz[2][:, 512:1024], po[0][:, :], po[1][:, :]]
        bname = ["B%d" % i for i in range(8)]

        def dma(eng, out, in_, reads=(), writes=()):
            return P.add(eng, DMA(out, in_), reads=reads, writes=writes, dma=True)

        def stage(name):
            if stop_after == name:
                P.barrier()
                P.frozen = True

        dma("pool", wcat_bf[:, C_KV:C_KV + 1024], wcat[:, C_KV:C_KV + 1024], writes=["wbf_0"])
        dma("pool", cst[:].rearrange("p a b -> p (a b)"), consts_in[:, :], writes=["cst"])
        dma("pool", masks[:].rearrange("p a b -> p (a b)"), masks_in[:, 0:4096], writes=["masks"])
        dma("pool", msw[:].rearrange("p a b -> p (a b)"), masks_in[:, 4096:4096 + 384], writes=["msw"])
        dma("sp", gain_bc[:], gains[0].partition_broadcast(128), writes=["gain"])
        dma("sp", fgain_bc[:], gains[1].partition_broadcast(128), writes=["fgain"])
        dma("sp", padb[:], padb_in[:, :], writes=["padb"])
        dma("sp", es_t[0:64, :], sinks[0].partition_broadcast(64), writes=["es_a"])
        dma("sp", es_t[64:128, :], sinks[1].partition_broadcast(64), writes=["es_b"])
        for c0 in range(1024, WC, 1024):
            c1 = min(WC, c0 + 1024)
            dma("pool", wcat_bf[:, c0:c1], wcat[:, c0:c1], writes=["wbf_%d" % c0])
        for r0 in range(0, 2560, 512):
            dma("pool", wb2_bf[r0:r0 + 512, :], wb2[r0:r0 + 512, :], writes=["wb2_%d" % r0])

        def wres_for(c0, c1):
            return sorted({"wbf_%d" % ((c // 1024) * 1024) for c in range(c0, c1, 128)})

        P.add("act", ACT(es_t[:], es_t[:], AF.Exp), reads=["es_a", "es_b"], writes=["es_t"])

        wv = wcat_bf.rearrange("(c p) n -> p c n", p=128)

        def load_unit(slot, c0, ncol=512):
            view = ring[slot][:, 0:8 * ncol].rearrange("p (c n) -> p c n", c=8)
            dma("sp", view, wv[:, :, c0:c0 + ncol], reads=wres_for(c0, c0 + ncol), writes=["ring%d" % slot])
            return view

        def load_unit_b2(slot, r0, nfc, c0):
            view = ring[slot][:, 0:nfc * 512].rearrange("p (c n) -> p c n", c=nfc)
            src = wb2_bf[r0:r0 + 128 * nfc, :].rearrange("(c p) n -> p c n", p=128)[:, :, c0:c0 + 512]
            rr = sorted({"wb2_%d" % ((r // 512) * 512) for r in range(r0, r0 + 128 * nfc, 128)})
            dma("sp", view, src, reads=rr, writes=["ring%d" % slot])
            return view

        stage("pro")
        xt_i = [0]

        def rms_scale(j, s):
            P.add("act", ACT(junk[:], xt[s][:], AF.Square, accum_out=ss[:, j:j + 1]),
                  reads=["xt%d" % s], writes=["junk", "ss%d" % j])
            P.add("act", ACT(rstd[:, j:j + 1], ss[:, j:j + 1], AF.Ln, scale=1.0 / D, bias=padb[:, 1:2]),
                  reads=["ss%d" % j, "padb"], writes=["rln%d" % j])
            P.add("act", ACT(rstd[:, j:j + 1], rstd[:, j:j + 1], AF.Exp, scale=-0.5),
                  reads=["rln%d" % j], writes=["rstd%d" % j])

        def norm_transpose(src_ap, nblk, xi):
            for j in range(nblk):
                s = xt_i[0] % 3
                xt_i[0] += 1
                dma("sp", xt[s][:], src_ap[j * 128:(j + 1) * 128, :], writes=["xt%d" % s])
                rms_scale(j, s)
                b = j % 2
                P.add("dve", STT(xnb[b][:], xt[s][:], rstd[:, j:j + 1], gain_bc[:], ALU.mult, ALU.mult),
                      reads=["xt%d" % s, "rstd%d" % j, "gain"], writes=["xnb%d" % b])
                bk = 6 + (j % 2)
                pb = banks[bk].bitcast(BF16)
                P.group("pe", [TR(pb[:, c * 128:(c + 1) * 128], xnb[b][:, c * 128:(c + 1) * 128], ident)
                               for c in range(KC)],
                        reads=["xnb%d" % b, "cst"], writes=[bname[bk]])
                P.add("act", ACT(xnT[xi][:, :, j * 128:(j + 1) * 128],
                                 pb[:, 0:1024].rearrange("p (c n) -> p c n", c=KC), AF.Copy),
                      reads=[bname[bk]], writes=["xnT%d_%d" % (xi, j)])

        def xres(xi, j0, j1):
            return ["xnT%d_%d" % (xi, j) for j in range(j0, j1)]

        evac_i = [0]

        def evac_copy(out_ap, in_ap, reads, writes, scale=None):
            k = evac_i[0]
            evac_i[0] += 1
            if k % 2 == 0:
                P.add("act", ACT(out_ap, in_ap, AF.Copy, scale=scale), reads=reads, writes=writes)
            elif scale is None:
                P.add("dve", CP(out_ap, in_ap), reads=reads, writes=writes)
            else:
                P.add("dve", TSMUL(out_ap, in_ap, scale), reads=reads, writes=writes)

        def proj_fm(w_view, col0, xT_ap, ntok, bk, reads):
            P.group("pe", [MM(banks[bk][:, 0:ntok], w_view[:, c, col0:col0 + 128], xT_ap[:, c, 0:ntok],
                              c == 0, c == KC - 1) for c in range(KC)],
                    reads=reads, writes=[bname[bk]])

        wkv0 = load_unit(0, C_KV)
        wkv1 = load_unit(1, C_KV + 512)
        groups = [(0, 1), (1, 4), (5, 4)] + ([(9, 4), (13, 4)] if NSTEP >= 2 else [])
        bki = [0]
        for gi, (b0, nblk) in enumerate(groups):
            ntok = nblk * 128
            xb = gi % 2
            st = gi % 2
            norm_transpose(h[b0 * 128:(b0 + nblk) * 128, :], nblk, xb)
            for hp in range(4):
                bk = bki[0] % 6
                bki[0] += 1
                proj_fm(wkv0, hp * 128, xnT[xb], ntok, bk, xres(xb, 0, nblk) + ["ring0"])
                evac_copy(KTst[st][:, hp, 0:ntok], banks[bk][:, 0:ntok], [bname[bk]], ["KTst%d_%d" % (st, hp)])
            for j in range(nblk):
                bk = bki[0] % 6
                bki[0] += 1
                P.group("pe", [MM(banks[bk][:, :], xnT[xb][:, c, j * 128:(j + 1) * 128], wkv1[:, c, :],
                                  c == 0, c == KC - 1) for c in range(KC)],
                        reads=xres(xb, j, j + 1) + ["ring1"], writes=[bname[bk]])
                evac_copy(Vst[st][:, j, :], banks[bk][:, :], [bname[bk]], ["Vst%d_%d" % (st, j)])
            t0 = b0 * 128
            dma("pool", KTs[:, :, t0:t0 + ntok].rearrange("h p t -> p h t"), KTst[st][:, :, 0:ntok],
                reads=["KTst%d_%d" % (st, hp) for hp in range(4)],
                writes=["KTs_b%d" % b for b in range(b0, b0 + nblk)])
            for hp in range(4):
                dma("pool", Vs[hp, :, t0:t0 + ntok].rearrange("p (j c) -> p j c", c=128),
                    Vst[st][:, 0:nblk, hp * 128:(hp + 1) * 128],
                    reads=["Vst%d_%d" % (st, j) for j in range(nblk)],
                    writes=["Vs_b%d_%d" % (b, hp) for b in range(b0, b0 + nblk)])

        stage("p1")

        def blk_group(kb):
            return 0 if kb == 0 else (kb - 1) // 4 + 1

        kv_i = [0]

        def fgroup(fns, reads, writes, split=True):
            if not split or len(fns) <= 4:
                P.group("pe", fns, reads=reads, writes=writes)
                return
            hlf = len(fns) // 2
            P.group("pe", fns[:hlf], reads=reads, writes=writes)
            yield
            P.group("pe", fns[hlf:], reads=reads, writes=writes)

        def kv_filler(m):
            dq = "pool"
            blocks = list(range(8 * m + 9, 8 * m + 17))
            w0 = ring[0][:, 0:4096].rearrange("p (c n) -> p c n", c=8)
            w1 = ring[1][:, 0:4096].rearrange("p (c n) -> p c n", c=8)
            dma(dq, w0, wv[:, :, C_KV:C_KV + 512], reads=wres_for(C_KV, C_KV + 512), writes=["ring0"])
            dma(dq, w1, wv[:, :, C_KV + 512:C_KV + 1024], reads=wres_for(C_KV + 512, C_KV + 1024), writes=["ring1"])
            slots = {}

            def xload(i):
                s = xt_i[0] % 3
                xt_i[0] += 1
                slots[i] = s
                b = blocks[i]
                dma(dq, xt[s][:], h[b * 128:(b + 1) * 128, :], writes=["xt%d" % s])

            xload(0)
            xload(1)
            yield
            bk = 7
            pb = banks[bk].bitcast(BF16)
            for i, b in enumerate(blocks):
                s = slots[i]
                st = i % 2
                rms_scale(5, s)
                P.add("dve", STT(xnb[st][:], xt[s][:], rstd[:, 5:6], gain_bc[:], ALU.mult, ALU.mult),
                      reads=["xt%d" % s, "rstd5", "gain"], writes=["xnb%d" % st])
                if i + 2 < len(blocks):
                    xload(i + 2)
                P.group("pe", [TR(pb[:, c * 128:(c + 1) * 128], xnb[st][:, c * 128:(c + 1) * 128], ident)
                               for c in range(KC)],
                        reads=["xnb%d" % st, "cst"], writes=[bname[bk]])
                P.add("dve", CP(xnT_f[:, :, :], pb[:, 0:1024].rearrange("p (c n) -> p c n", c=KC)),
                      reads=[bname[bk]], writes=["xnT_f"])
                yield
                for hh in range(2):
                    fns = []
                    for q_ in range(2):
                        hp = 2 * hh + q_
                        fns += [MM(banks[bk][:, q_ * 128:(q_ + 1) * 128], w0[:, c, hp * 128:(hp + 1) * 128],
                                   xnT_f[:, c, :], c == 0, c == KC - 1) for c in range(KC)]
                    yield from fgroup(fns, ["xnT_f", "ring0"], [bname[bk]])
                    P.add("dve", CP(KTst_f[st][:, 2 * hh:2 * hh + 2, :],
                                    banks[bk][:, 0:256].rearrange("p (a n) -> p a n", a=2)),
                          reads=[bname[bk]], writes=["KTst_f%d_%d" % (st, hh)])
                    yield
                yield from fgroup([MM(banks[bk][:, :], xnT_f[:, c, :], w1[:, c, :], c == 0, c == KC - 1)
                                   for c in range(KC)], ["xnT_f", "ring1"], [bname[bk]])
                P.add("dve", CP(Vst_f[st][:, :], banks[bk][:, :]), reads=[bname[bk]], writes=["Vst_f%d" % st])
                dma(dq, KTs[:, :, b * 128:(b + 1) * 128].rearrange("h p t -> p h t"), KTst_f[st][:, :, :],
                    reads=["KTst_f%d_0" % st, "KTst_f%d_1" % st], writes=["KTs_b%d" % b])
                for hp in range(4):
                    dma(dq, Vs[hp, :, b * 128:(b + 1) * 128], Vst_f[st][:, hp * 128:(hp + 1) * 128],
                        reads=["Vst_f%d" % st], writes=["Vs_b%d_%d" % (b, hp)])
                yield

        def front_end(m, banks_ok, dq, dve_only):
            par = m % 2
            own = hown[m * 640:(m + 1) * 640, :]
            xo = xnT[par]
            xq = xo[:, :, 128:640]
            xq_res = xres(par, 1, 5)
            xo_res = xres(par, 0, 5)
            bi = [0]

            def nb():
                b = banks_ok[bi[0] % len(banks_ok)]
                bi[0] += 1
                return b

            def ev(out_ap, in_ap, reads, writes, scale=None):
                if not dve_only:
                    evac_copy(out_ap, in_ap, reads, writes, scale)
                elif scale is None:
                    P.add("dve", CP(out_ap, in_ap), reads=reads, writes=writes)
                else:
                    P.add("dve", TSMUL(out_ap, in_ap, scale), reads=reads, writes=writes)

            def lu(slot, c0, ncol=512):
                view = ring[slot][:, 0:8 * ncol].rearrange("p (c n) -> p c n", c=8)
                dma(dq, view, wv[:, :, c0:c0 + ncol], reads=wres_for(c0, c0 + ncol), writes=["ring%d" % slot])
                return view

            slots = {}

            def xload(j):
                s = xt_i[0] % 3
                xt_i[0] += 1
                slots[j] = s
                dma(dq, xt[s][:], own[j * 128:(j + 1) * 128, :], writes=["xt%d" % s])

            xload(0)
            xload(1)
            wq = lu(0, C_Q)
            wk = lu(1, C_SWK)
            wvv = lu(2, C_SWV, ncol=128)
            wa0 = lu(3, C_SWQA)
            dma(dq, tk[:, 0, :], tabk[0, :, m * 640:(m + 1) * 640], writes=["tk0"])
            dma(dq, tk[:, 1, :], tabk[1, :, m * 640:(m + 1) * 640], writes=["tk1"])
            yield
            for j in range(5):
                s = slots[j]
                rms_scale(j, s)
                b = j % 2
                P.add("dve", STT(xnb[b][:], xt[s][:], rstd[:, j:j + 1], gain_bc[:], ALU.mult, ALU.mult),
                      reads=["xt%d" % s, "rstd%d" % j, "gain"], writes=["xnb%d" % b])
                if j + 2 < 5:
                    xload(j + 2)
                bk = nb()
                pb = banks[bk].bitcast(BF16)
                P.group("pe", [TR(pb[:, c * 128:(c + 1) * 128], xnb[b][:, c * 128:(c + 1) * 128], ident)
                               for c in range(KC)],
                        reads=["xnb%d" % b, "cst"], writes=[bname[bk]])
                ev(xo[:, :, j * 128:(j + 1) * 128], pb[:, 0:1024].rearrange("p (c n) -> p c n", c=KC),
                   [bname[bk]], ["xnT%d_%d" % (par, j)])
                yield
            for hp in range(4):
                bk = nb()
                yield from fgroup([MM(banks[bk][:, 0:512], wq[:, c, hp * 128:(hp + 1) * 128], xq[:, c, 0:512],
                                      c == 0, c == KC - 1) for c in range(KC)],
                                  xq_res + ["ring0"], [bname[bk]], dve_only)
                ev(qT2[par][:, hp, :], banks[bk][:, :], [bname[bk]], ["qT%d_%d" % (par, hp)], scale=0.125)
                yield
            wb0 = lu(0, C_SWQB)
            for g in range(2):
                for (n0, n1) in ((0, 512), (512, 640)):
                    for which, rt, tname in ((0, rt1, "tk0"), (1, rt2, "tk1")):
                        col = which * 256 + g * 128
                        bk = nb()
                        yield from fgroup([MM(banks[bk][:, 0:n1 - n0], wk[:, c, col:col + 128], xo[:, c, n0:n1],
                                              c == 0, c == KC - 1) for c in range(KC)],
                                          xo_res + ["ring1"], [bname[bk]], dve_only and n1 - n0 > 128)
                        P.add("dve", TT(rt[:, n0:n1], banks[bk][:, 0:n1 - n0], tk[:, which, n0:n1], ALU.mult),
                              reads=[bname[bk], tname], writes=["rt%d" % (which + 1)])
                        if which == 1:
                            P.add("dve", TT(kt_sw[:, g, n0:n1], rt1[:, n0:n1], rt2[:, n0:n1], ALU.add),
                                  reads=["rt1", "rt2"], writes=["kt_sw"])
                        yield
            wa1 = lu(1, C_SWQA + 512)
            bk = nb()
            for j in range(4):
                P.group("pe", [MM(banks[bk][:, j * 128:(j + 1) * 128], xo[:, c, j * 128:(j + 1) * 128], wvv[:, c, :],
                                  c == 0, c == KC - 1) for c in range(KC)],
                        reads=xo_res + ["ring2"], writes=[bname[bk]])
            ev(v_sw[:, 0:4, :], banks[bk][:, :].rearrange("p (j c) -> p j c", j=4), [bname[bk]], ["v_sw_a"])
            yield
            bk = nb()
            P.group("pe", [MM(banks[bk][:, 0:128], xo[:, c, 512:640], wvv[:, c, :], c == 0, c == KC - 1)
                           for c in range(KC)],
                    reads=xo_res + ["ring2"], writes=[bname[bk]])
            ev(v_sw[:, 4, :], banks[bk][:, 0:128], [bname[bk]], ["v_sw_b"])
            yield
            wb1 = lu(2, C_SWQB + 512)
            for half, (wa, ra, wb_, rb) in enumerate(((wa0, "ring3", wb0, "ring0"), (wa1, "ring1", wb1, "ring2"))):
                for cc in range(4):
                    cq = half * 4 + cc
                    for (wvw, rr, rt, which) in ((wa, ra, rt1, 0), (wb_, rb, rt2, 1)):
                        bk = nb()
                        yield from fgroup([MM(banks[bk][:, 0:512], wvw[:, c, cc * 128:(cc + 1) * 128],
                                              xq[:, c, 0:512], c == 0, c == KC - 1) for c in range(KC)],
                                          xq_res + [rr], [bname[bk]], dve_only)
                        P.add("dve", STT(rt[:, 0:512], banks[bk][:, :], 0.125, tk[:, which, 128:640],
                                         ALU.mult, ALU.mult),
                              reads=[bname[bk], "tk%d" % which], writes=["rt%d" % (which + 1)])
                        if which == 1:
                            P.add("dve", TT(qT_sw[:, cq, :], rt1[:, 0:512], rt2[:, 0:512], ALU.add),
                                  reads=["rt1", "rt2"], writes=["qT_sw%d" % cq])
                        yield

        for _ in front_end(0, list(range(8)), "sp", False):
            pass
        P.barrier()

        for m in range(NSTEP):
            par = m % 2
            xo = xnT[par]
            xq = xo[:, :, 128:640]
            xq_res = xres(par, 1, 5)
            own = hown[m * 640:(m + 1) * 640, :]
            qT = qT2[par]
            stage("proj")
            for g in range(2):
                for j in range(4):
                    for kb2 in range(2):
                        fns = []
                        for h8 in range(8):
                            cq = 4 * g + h8 // 2
                            hf = h8 % 2
                            zc = hf * 512 + (h8 // 2) * 128
                            fns.append(MM(pz[kb2][:, zc:zc + 128],
                                          kt_sw[hf * 64:(hf + 1) * 64, g, (j + kb2) * 128:(j + kb2 + 1) * 128],
                                          qT_sw[hf * 64:(hf + 1) * 64, cq, j * 128:(j + 1) * 128],
                                          h8 < 2, True, True))
                        mi = (2 if (m == 0 and j == 0) else 0) if kb2 == 0 else 1
                        for h8 in range(8):
                            fns.append(MM(pz[kb2][:, h8 * 128:(h8 + 1) * 128], ident, msw[:, mi, :], False, True, True))
                        P.group("pe", fns, reads=["kt_sw", "msw", "cst"] + ["qT_sw%d" % (4 * g + a) for a in range(4)],
                                writes=["B%d" % (2 * kb2), "B%d" % (2 * kb2 + 1)])
                        P.add("act", ACT(Pb[:, kb2 * 1024:(kb2 + 1) * 1024], pz[kb2][:, :], AF.Exp),
                              reads=["B%d" % (2 * kb2), "B%d" % (2 * kb2 + 1)], writes=["Pb%d" % kb2])
                    stage("swa1")
                    for which, bk in ((0, 4), (1, 5)):
                        fns = []
                        for hf in range(2):
                            for kb2 in range(2):
                                lhs = v_sw[:, j + kb2, g * 64:(g + 1) * 64] if which == 0 else ones[:, 0:64]
                                rhs = Pb[:, kb2 * 1024 + hf * 512:kb2 * 1024 + (hf + 1) * 512].rearrange(
                                    "p (a n) -> p a n", a=4)
                                fns.append(MM(banks[bk][hf * 64:(hf + 1) * 64, :].rearrange("p (a n) -> p a n", a=4),
                                              lhs, rhs, kb2 == 0, kb2 == 1))
                        P.group("pe", fns, reads=["Pb0", "Pb1", "v_sw_a", "v_sw_b", "cst"], writes=[bname[bk]])
                    stage("swa2")
                    P.add("dve", TT(rt1[:, 0:512].rearrange("p (a n) -> p a n", a=4),
                                    banks[5][:, :].rearrange("p (a n) -> p a n", a=4),
                                    es_t[:, 4 * g:4 * g + 4].unsqueeze(2).to_broadcast([128, 4, 128]), ALU.add),
                          reads=["B5", "es_t"], writes=["rt1"])
                    P.add("act", ACT(rt2[:, 0:512], rt1[:, 0:512], AF.Ln), reads=["rt1"], writes=["rt2"])
                    P.add("act", ACT(rt2[:, 0:512], rt2[:, 0:512], AF.Exp, scale=-1.0), reads=["rt2"], writes=["rt2"])
                    P.add("dve", TT(oT_sw[:, 4 * g:4 * g + 4, j * 128:(j + 1) * 128],
                                    banks[4][:, :].rearrange("p (a n) -> p a n", a=4),
                                    rt2[:, 0:512].rearrange("p (a n) -> p a n", a=4), ALU.mult),
                          reads=["B4", "rt2"], writes=["oT_sw"])

            stage("swa")
            its = []
            kmax = 8 * m + 8
            for hp in range(4):
                for kb in range(kmax, -1, -1):
                    rel = kb - 8 * m
                    its.append(dict(hp=hp, kb=kb, first=(kb == kmax), last=(kb == 0),
                                    c0=(128 * (rel - 5) if rel >= 5 else 0),
                                    mi=(rel - 1 if rel >= 1 else None)))
            loads = []
            for i, it in enumerate(its):
                ch = 0 if it["kb"] == 0 else (it["kb"] - 1) // 8 + 1
                it["kl"] = 0 if it["kb"] == 0 else (it["kb"] - 1) % 8
                if not loads or loads[-1][1:] != (it["hp"], ch):
                    loads.append((i, it["hp"], ch))
                it["ld"] = len(loads) - 1
            last_it = {}
            for i, it in enumerate(its):
                last_it[it["ld"]] = i
            slot_of = {}
            nl = [0]

            def issue_loads(t_done):
                while nl[0] < len(loads) and (nl[0] < 4 or last_it[nl[0] - 4] <= t_done):
                    _, hp, ch = loads[nl[0]]
                    slot = kv_i[0] % 4
                    kv_i[0] += 1
                    slot_of[nl[0]] = slot
                    tb0, nblk = (0, 1) if ch == 0 else (8 * (ch - 1) + 1, 8)
                    gr = ["KTs_b%d" % b for b in range(tb0, tb0 + nblk)]
                    gv = ["Vs_b%d_%d" % (b, hp) for b in range(tb0, tb0 + nblk)]
                    dma("sp", kvr[slot][:, 0:nblk * 128], KTs[hp, :, tb0 * 128:(tb0 + nblk) * 128],
                        reads=gr, writes=["kvK%d" % slot])
                    dma("sp", kvr[slot][:, 1024:1024 + nblk * 128], Vs[hp, :, tb0 * 128:(tb0 + nblk) * 128],
                        reads=gv, writes=["kvV%d" % slot])
                    nl[0] += 1

            def zres(t):
                k = t % 3
                return ["B%d" % (2 * k), "B%d" % (2 * k + 1)]

            def zv(t, c0):
                return pz[t % 3][:, :].rearrange("p (e n) -> p e n", e=2)[:, :, c0:512]

            def st_QK(t):
                it = its[t]
                slot = slot_of[it["ld"]]
                c0, kl, hp = it["c0"], it["kl"], it["hp"]
                fns = [MM(pz[t % 3][:, e_ * 512 + c0:(e_ + 1) * 512],
                          kvr[slot][e_ * 64:(e_ + 1) * 64, kl * 128:(kl + 1) * 128],
                          qT[e_ * 64:(e_ + 1) * 64, hp, c0:512], True, True, True) for e_ in range(2)]
                if it["mi"] is not None:
                    fns += [MM(pz[t % 3][:, e_ * 512 + c0:(e_ + 1) * 512], ident, masks[:, it["mi"], c0:512],
                               False, True, True) for e_ in range(2)]
                P.group("pe", fns, reads=["kvK%d" % slot, "qT%d_%d" % (par, hp), "masks", "cst"],
                        writes=zres(t))

            def st_E(t):
                it = its[t]
                c0 = it["c0"]
                bias = padb[:, 0:1] if it["kb"] == 0 else None
                P.add("act", ACT(Eb[:, :, c0:512], zv(t, c0), AF.Exp, bias=bias),
                      reads=zres(t) + ["padb"], writes=["Eb"])

            def st_L(t):
                c0 = its[t]["c0"]
                P.add("act", ACT(Lb[t % 2][:, :, c0:512], Eb[:, :, c0:512], AF.Ln, bias=1.0),
                      reads=["Eb"], writes=["Lb%d" % (t % 2)])

            def st_ADD(t):
                it = its[t]
                c0 = it["c0"]
                if it["first"]:
                    P.add("dve", MSET(Rb[:], 0.0), writes=["Rb"])
                if it["last"]:
                    return
                P.add("dve", TT(Rb[:, :, c0:512], Rb[:, :, c0:512], Lb[t % 2][:, :, c0:512], ALU.add),
                      reads=["Lb%d" % (t % 2), "Rb"], writes=["Rb"])

            def st_CS(t):
                it = its[t]
                c0 = it["c0"]
                fns = []
                for e_ in range(2):
                    o = pz[t % 3][:, e_ * 512 + c0:(e_ + 1) * 512]
                    fns.append(MM(o, negU, Lb[t % 2][:, e_, c0:512], False, True, True))
                    if not it["first"]:
                        fns.append(MM(o, negOnes, Rb[:, e_, c0:512], False, True, True))
                P.group("pe", fns, reads=["Lb%d" % (t % 2), "cst"] + ([] if it["first"] else ["Rb"]),
                        writes=zres(t))

            def st_W(t):
                it = its[t]
                c0 = it["c0"]
                bias = padb[:, 0:1] if it["kb"] == 0 else None
                P.add("act", ACT(Wb[t % 2][:, :, c0:512], zv(t, c0), AF.Exp, bias=bias),
                      reads=zres(t) + ["padb"], writes=["Wb%d" % (t % 2)])

            def st_PV(t):
                it = its[t]
                c0, kl, hp = it["c0"], it["kl"], it["hp"]
                slot = slot_of[it["ld"]]
                ob = 0
                fns = []
                for e_ in range(2):
                    vcol = 1024 + kl * 128 + e_ * 64
                    fns.append(MM(po[ob][e_ * 64:(e_ + 1) * 64, c0:512], kvr[slot][:, vcol:vcol + 64],
                                  Wb[t % 2][:, e_, c0:512], it["first"], True, True))
                P.group("pe", fns, reads=["Wb%d" % (t % 2), "kvV%d" % slot, "masks", "cst"], writes=["B6"])
                if it["last"]:
                    P.add("dve", CP(oT_sb[:, hp, :], po[ob][:, :]), reads=["B6"], writes=["oT_sb%d" % hp])

            n = len(its)
            issue_loads(-1)
            def both(m_):
                if m_ >= 1:
                    yield from kv_filler(m_)
                yield from front_end(m_ + 1, [7], "pool", True)
            filler = both(m) if m + 1 < NSTEP else None
            n_items = 140 if m >= 1 else 72
            stride = max(1, n // n_items)
            per_tick = 1
            pre = {}

            def prefetch_backend():
                pre["wz"] = load_unit(0, C_ZSB)
                pre["wzw"] = [load_unit(1, C_ZSW), load_unit(2, C_ZSW + 512)]
                pre["wbs0"] = load_unit_b2(3, 0, 4, 0)
            if filler is None:
                prefetch_backend()
            for t in range(-3, n):
                if 0 <= t + 2 < n:
                    st_E(t + 2)
                if 0 <= t + 1 < n:
                    st_CS(t + 1)
                    st_ADD(t + 1)
                if 0 <= t < n:
                    st_W(t)
                if 0 <= t + 3 < n:
                    st_QK(t + 3)
                if 0 <= t < n:
                    st_PV(t)
                    issue_loads(t)
                if 0 <= t + 2 < n:
                    st_L(t + 2)
                if filler is not None and t >= 0 and t % stride == 0:
                    try:
                        for _ in range(per_tick):
                            next(filler)
                    except StopIteration:
                        filler = None
                        prefetch_backend()
            if filler is not None:
                for _ in filler:
                    pass
                prefetch_backend()

            P.barrier()
            stage("sb")
            wz = pre["wz"]
            wzw = pre["wzw"]
            for ci in range(12):
                if ci < 4:
                    wvw, rr, col = wz, "ring0", ci * 128
                    osrc, odst, ores, dres = oT_sb[:, ci, :], gT_sb[:, ci, :], "oT_sb%d" % ci, "gT_sb%d" % ci
                else:
                    cq = ci - 4
                    wvw, rr, col = wzw[cq // 4], "ring%d" % (1 + cq // 4), (cq % 4) * 128
                    osrc, odst, ores, dres = oT_sw[:, cq, :], gT_sw[:, cq, :], "oT_sw", "gT_sw%d" % cq
                bk = ci % 4
                proj_fm(wvw, col, xq, 512, bk, xq_res + [rr])
                P.add("act", ACT(sg[bk][:], banks[bk][:, :], AF.Sigmoid), reads=[bname[bk]], writes=["sg%d" % bk])
                P.add("dve", TT(sg[bk][:], sg[bk][:], banks[bk][:, :], ALU.mult),
                      reads=[bname[bk], "sg%d" % bk], writes=["sg%d" % bk])
                P.add("dve", TT(odst, sg[bk][:], osrc, ALU.mult), reads=["sg%d" % bk, ores], writes=[dres])

            stage("gate")
            for hh in range(2):
                wbs = pre["wbs0"] if hh == 0 else load_unit_b2(3, 0, 4, hh * 512)
                wbw = load_unit_b2(0, 512, 8, hh * 512)
                wgs = load_unit(1, C_GSB + hh * 512)
                wgw = load_unit(2, C_GSW + hh * 512)
                for cc in range(4):
                    c_out = hh * 4 + cc
                    q4 = (cc % 2) * 4
                    b_ys, b_yw, b_gs, b_gw = q4, q4 + 1, q4 + 2, q4 + 3
                    P.group("pe", [MM(banks[b_ys][:, :], wbs[:, fc, cc * 128:(cc + 1) * 128], gT_sb[:, fc, :],
                                      fc == 0, fc == 3) for fc in range(4)],
                            reads=["ring3"] + ["gT_sb%d" % k for k in range(4)], writes=[bname[b_ys]])
                    P.group("pe", [MM(banks[b_yw][:, :], wbw[:, fc, cc * 128:(cc + 1) * 128], gT_sw[:, fc, :],
                                      fc == 0, fc == 7) for fc in range(8)],
                            reads=["ring0"] + ["gT_sw%d" % k for k in range(8)], writes=[bname[b_yw]])
                    proj_fm(wgs, cc * 128, xq, 512, b_gs, xq_res + ["ring1"])
                    proj_fm(wgw, cc * 128, xq, 512, b_gw, xq_res + ["ring2"])
                    ia, ib = (cc % 2) * 2, (cc % 2) * 2 + 1
                    sA, sB, rA, rB = sg[ia], sg[ib], "sg%d" % ia, "sg%d" % ib
                    tt, rtt = t1[cc % 2], "t1_%d" % (cc % 2)
                    P.add("act", ACT(sA[:], banks[b_gs][:, :], AF.Sigmoid), reads=[bname[b_gs]], writes=[rA])
                    P.add("act", ACT(sB[:], banks[b_gw][:, :], AF.Sigmoid), reads=[bname[b_gw]], writes=[rB])
                    P.add("dve", TT(tt[:], sA[:], banks[b_ys][:, :], ALU.mult), reads=[rA, bname[b_ys]], writes=[rtt])
                    P.add("dve", TT(sB[:], sB[:], banks[b_yw][:, :], ALU.mult), reads=[rB, bname[b_yw]], writes=[rB])
                    P.add("dve", TT(mT[:, c_out, :], tt[:], sB[:], ALU.add), reads=[rB, rtt], writes=["mT%d" % c_out])

            stage("merge")
            wo = [load_unit_b2(3, 1536, 8, 0), load_unit_b2(0, 1536, 8, 512)]
            wor = ["ring3", "ring0"]
            for j in range(4):
                s = xt_i[0] % 3
                xt_i[0] += 1
                dma("sp", xt[s][:], own[128 + j * 128:128 + (j + 1) * 128, :], writes=["xt%d" % s])
                for hh in range(2):
                    bk = (j % 2) * 2 + hh
                    P.group("pe", [MM(banks[bk][:, :], mT[:, c, j * 128:(j + 1) * 128], wo[hh][:, c, :],
                                      c == 0, c == KC - 1) for c in range(KC)],
                            reads=[wor[hh]] + ["mT%d" % k for k in range(8)], writes=[bname[bk]])
                    P.add("dve", TT(xt[s][:, hh * 512:(hh + 1) * 512], xt[s][:, hh * 512:(hh + 1) * 512],
                                    banks[bk][:, :], ALU.add),
                          reads=[bname[bk], "xt%d" % s], writes=["xt%d" % s])
                rms_scale(j, s)
                ob = j % 2
                P.add("dve", STT(outt[ob][:], xt[s][:], rstd[:, j:j + 1], fgain_bc[:], ALU.mult, ALU.mult),
                      reads=["xt%d" % s, "rstd%d" % j, "fgain"], writes=["outt%d" % ob])
                r0 = m * 512 + j * 128
                dma("pool", y[r0:r0 + 128, :], outt[ob][:], reads=["outt%d" % ob], writes=["y_%d_%d" % (m, j)])
            P.barrier()

        sem_stack = ExitStack()
        with sem_stack:
            def semctx(name):
                return sem_stack.enter_context(nc.semaphore(name))
            state = {}
            _prepare(P, semctx, state)
            with nc.Block() as block:
                block.tensor(lambda eng: _emit_engine(P, "pe", eng, state))
                block.scalar(lambda eng: _emit_engine(P, "act", eng, state))
                block.vector(lambda eng: _emit_engine(P, "dve", eng, state))
                block.gpsimd(lambda eng: _emit_engine(P, "pool", eng, state))
                block.sync(lambda eng: _emit_engine(P, "sp", eng, state))
    build.last_stats = dict(n_inst={e: len(P.q[e]) for e in P.ENGS}, maxcount=state["maxcount"])
    return nc


def _host_prep(x, meta_tokens, norm_gain, w_in, w_branch_sb, w_branch_swa, w_out, attn_sinks, final_norm_gain):
    B, SEQ, _ = x.shape
    NSTEP = SEQ // 1024
    NB = 8 * NSTEP + 1
    f32 = np.float32
    w = np.asarray(w_in[0], f32)
    offs = np.cumsum([0, 512, 512, 512, 1024, 128, 128, 512, 1024, 1024, 1024])
    sb_q, sb_k, sb_v, sw_q, sw_k, sw_v, sb_z, sw_z, g_sb, g_sw = [w[:, offs[i]:offs[i + 1]] for i in range(10)]
    perm64 = np.concatenate([np.arange(32, 64), np.arange(0, 32)])
    kA = np.concatenate([sw_k[:, 0:64], sw_k[:, 0:64], sw_k[:, 64:128], sw_k[:, 64:128]], axis=1)
    kB0 = sw_k[:, 0:64][:, perm64]
    kB1 = sw_k[:, 64:128][:, perm64]
    kB = np.concatenate([kB0, kB0, kB1, kB1], axis=1)
    qperm = np.concatenate([hh * 64 + perm64 for hh in range(16)])
    wcat = np.concatenate([sb_k, sb_v, sb_q, kA, kB, sw_v, sw_q, sw_q[:, qperm], sb_z, sw_z, g_sb, g_sw], axis=1)
    assert wcat.shape[1] == WC
    wcat = np.ascontiguousarray(wcat, f32)
    wb2 = np.ascontiguousarray(np.concatenate([w_branch_sb[0], w_branch_swa[0], w_out[0]], axis=0), f32)
    gains = np.ascontiguousarray(np.stack([norm_gain[0], final_norm_gain]), f32)
    sk = np.asarray(attn_sinks[0], f32)
    sinks = np.ascontiguousarray(np.stack([sk[0::2], sk[1::2]]), f32)
    half = 32
    inv = 10000.0 ** (-np.arange(half, dtype=np.float64) / half)
    dd = np.arange(128) % 64
    fi = dd % 32
    sign = np.where(dd < 32, -1.0, 1.0).astype(f32)

    def tables(pos):
        ang = pos.astype(np.float64)[None, :] * inv[fi][:, None]
        return np.cos(ang).astype(f32), (np.sin(ang) * sign[:, None]).astype(f32)

    consts = np.zeros((128, 5, 128), f32)
    consts[:, 0, :] = np.eye(128, dtype=f32)
    jj, s_ = np.meshgrid(np.arange(128), np.arange(128), indexing="ij")
    consts[:, 1, :] = np.where(jj >= s_, -1.0, 0.0)
    consts[:, 2, :] = -1.0
    consts[:, 3, :] = 0.0
    consts[:, 4, :] = 1.0
    consts = consts.reshape(128, 640)
    padb = np.zeros((128, 2), f32)
    padb[:PAD, 0] = NEG
    padb[:, 1] = EPS
    in_maps = []
    for b in range(B):
        hb = np.concatenate([np.zeros((PAD, D), f32), np.asarray(meta_tokens, f32), np.asarray(x[b], f32)], axis=0)
        for r in range(2):
            rows = []
            posk = []
            posq = []
            for m in range(NSTEP):
                first = 8 * m + 1 + 4 * r
                rows.append(hb[(first - 1) * 128:(first + 4) * 128])
                p = np.arange((first - 1) * 128, (first + 4) * 128) - PAD
                posk.append(p)
                posq.append(p[128:])
            hown = np.ascontiguousarray(np.concatenate(rows, axis=0))
            ck, sk_ = tables(np.concatenate(posk))
            tabk = np.ascontiguousarray(np.stack([ck, sk_]))
            mk = np.zeros((128, 8, 512), f32)
            s_l = np.arange(128)[:, None]
            t_l = np.arange(128)[None, :]
            tri = np.where(s_l < t_l, 0.0, NEG).astype(f32)
            for rel in range(1, 9):
                for qi in range(4):
                    qrel = 1 + 4 * r + qi
                    if rel < qrel:
                        blkm = 0.0
                    elif rel == qrel:
                        blkm = tri
                    else:
                        blkm = NEG
                    mk[:, rel - 1, qi * 128:(qi + 1) * 128] = blkm
            msw = np.zeros((128, 3, 128), f32)
            msw[:, 0, :] = np.where(s_l > t_l, 0.0, NEG)
            msw[:, 1, :] = np.where(s_l <= t_l, 0.0, NEG)
            msw[:, 2, :] = msw[:, 0, :]
            if r == 0:
                msw[:PAD, 2, :] = NEG
            masks = np.ascontiguousarray(np.concatenate([mk.reshape(128, 4096), msw.reshape(128, 384)], axis=1))
            in_maps.append(dict(h=np.ascontiguousarray(hb), hown=hown, wcat=wcat, wb2=wb2, gains=gains, sinks=sinks,
                                tabk=tabk, masks=masks, consts=consts, padb=padb))
    return NSTEP, in_maps


_NC_CACHE = {}


def run(x, meta_tokens, norm_gain, w_in, w_branch_sb, w_branch_swa, w_out, attn_sinks, final_norm_gain, debug=False,
        trace=False, stop_after=None):
    x = np.asarray(x)
    B, SEQ, _ = x.shape
    NSTEP, in_maps = _host_prep(x, meta_tokens, norm_gain, w_in, w_branch_sb, w_branch_swa, w_out, attn_sinks,
                                final_norm_gain)
    key = (NSTEP, debug, stop_after)
    if key not in _NC_CACHE:
        _NC_CACHE[key] = build(NSTEP, debug, stop_after)
    nc = _NC_CACHE[key]
    res = run_bass_kernel_spmd(nc, in_maps, core_ids=list(range(2 * B)), trace=trace)
    out = np.zeros((B, SEQ, D), np.float32)
    for b in range(B):
        for r in range(2):
            yy = res.results[2 * b + r]["y"]
            for m in range(NSTEP):
                first = 8 * m + 1 + 4 * r
                out[b, (first - 1) * 128:(first + 3) * 128] = yy[m * 512:(m + 1) * 512]
    if debug:
        return out, res
    return out


def kernel(x, meta_tokens, norm_gain, w_in, w_branch_sb, w_branch_swa, w_out, attn_sinks, final_norm_gain):
    return run(np.asarray(x), np.asarray(meta_tokens), np.asarray(norm_gain), np.asarray(w_in),
               np.asarray(w_branch_sb), np.asarray(w_branch_swa), np.asarray(w_out), np.asarray(attn_sinks),
               np.asarray(final_norm_gain))
```

```python
import numpy as np
import concourse.bass as bass
import concourse.mybir as mybir
from concourse.bass_utils import run_bass_kernel_spmd
from contextlib import ExitStack

F32 = mybir.dt.float32
BF16 = mybir.dt.bfloat16
AF = mybir.ActivationFunctionType
ALU = mybir.AluOpType

D = 1024
KC = 8
N_META = 16
PAD = 112
EPS = 1e-6
NEG = -30000.0

C_KV, C_Q, C_SWK, C_SWV, C_SWQA, C_SWQB, C_ZSB, C_ZSW, C_GSB, C_GSW, WC = (
    0, 1024, 1536, 2048, 2176, 3200, 4224, 4736, 5760, 6784, 7808)


class _Inst:
    __slots__ = ("eng", "fn", "deps", "dma", "needed", "semkey", "val", "idx")

    def __init__(self, eng, fn, deps, dma):
        self.eng, self.fn, self.deps, self.dma = eng, fn, deps, dma
        self.needed = False
        self.semkey = None
        self.val = 0


class Prog:
    ENGS = ("pe", "act", "dve", "pool", "sp")

    def __init__(self):
        self.q = {e: [] for e in self.ENGS}
        self.res = {}
        self.order = []
        self.bar_from = 0
        self.frozen = False

    def add(self, eng, fn, reads=(), writes=(), dma=False, extra=()):
        if self.frozen:
            return None
        deps = []
        for r in reads:
            lw, _ = self.res.get(r, (None, None))
            if lw is not None:
                deps.append(lw)
        for w in writes:
            lw, rd = self.res.get(w, (None, []))
            if lw is not None:
                deps.append(lw)
            if rd:
                deps.extend(rd)
        deps.extend(extra)
        ins = _Inst(eng, fn, deps, dma)
        ins.idx = len(self.order)
        self.order.append(ins)
        self.q[eng].append(ins)
        for r in reads:
            lw, rd = self.res.get(r, (None, []))
            self.res[r] = (lw, rd + [ins])
        for w in writes:
            self.res[w] = (ins, [])
        return ins

    def group(self, eng, fns, reads=(), writes=()):
        n = len(fns)
        for i, fn in enumerate(fns):
            if i == 0 or i == n - 1:
                self.add(eng, fn, reads=reads, writes=writes)
            else:
                self.add(eng, fn)

    def barrier(self):
        if self.frozen:
            return
        lasts = [self.q[e][-1] for e in self.ENGS if self.q[e]]
        dmas = [i for i in self.order[self.bar_from:] if i.dma]
        self.bar_from = len(self.order)
        for e in self.ENGS:
            self.add(e, None, extra=lasts + dmas)


def _prepare(P, semctx, state):
    NSEM = 20
    for e in P.ENGS:
        k = 0
        last_on = {}
        for ins in P.q[e]:
            if ins.fn is not None and ins.dma:
                ins.semkey = ("dma", e, k % NSEM)
                k += 1
                prev = last_on.get(ins.semkey)
                if prev is not None:
                    ins.deps.append(prev)
                last_on[ins.semkey] = ins
    for ins in P.order:
        for d in ins.deps:
            if d is ins or d.fn is None:
                continue
            if (not d.dma) and d.eng == "pe" and ins.eng == "pe":
                continue
            d.needed = True
    counts = {}
    for e in P.ENGS:
        for ins in P.q[e]:
            if ins.fn is None:
                continue
            if ins.dma:
                pass
            elif ins.needed:
                ins.semkey = ("eng", e)
            else:
                continue
            counts[ins.semkey] = counts.get(ins.semkey, 0) + (16 if ins.dma else 1)
            ins.val = counts[ins.semkey]
    state["sems"] = {k: semctx("s_" + "_".join(str(x) for x in k)) for k in counts}
    state["prepared"] = True
    state["maxcount"] = max(counts.values()) if counts else 0


def _emit_engine(P, e, eng, state):
    sems = state["sems"]
    waited = {}
    for ins in P.q[e]:
        need = {}
        for d in ins.deps:
            if d.fn is None or d.semkey is None or d is ins:
                continue
            if (not d.dma) and d.eng == "pe" and e == "pe":
                continue
            if waited.get(d.semkey, 0) >= d.val:
                continue
            if need.get(d.semkey, 0) < d.val:
                need[d.semkey] = d.val
        for k, v in need.items():
            eng.wait_ge(sems[k], v)
            waited[k] = v
        if ins.fn is None:
            continue
        r = ins.fn(eng)
        if ins.semkey is not None:
            r.then_inc(sems[ins.semkey], 16 if ins.dma else 1)


def MM(out, lhsT, rhs, start, stop, skip=False):
    if skip:
        return lambda e: e.matmul(out, lhsT, rhs, start=start, stop=stop, skip_group_check=True)
    return lambda e: e.matmul(out, lhsT, rhs, start=start, stop=stop)


def TR(out, in_, ident):
    return lambda e: e.transpose(out, in_, ident)


def ACT(out, in_, func, bias=None, scale=None, accum_out=None):
    kw = {}
    if bias is not None:
        kw["bias"] = bias
    if scale is not None:
        kw["scale"] = scale
    if accum_out is not None:
        kw["accum_out"] = accum_out
    return lambda e: e.activation(out=out, in_=in_, func=func, **kw)


def TT(out, in0, in1, op):
    return lambda e: e.tensor_tensor(out=out, in0=in0, in1=in1, op=op)


def STT(out, in0, scalar, in1, op0, op1):
    return lambda e: e.scalar_tensor_tensor(out=out, in0=in0, scalar=scalar, in1=in1, op0=op0, op1=op1)


def TSMUL(out, in0, scalar):
    return lambda e: e.tensor_scalar(out=out, in0=in0, scalar1=scalar, scalar2=None, op0=ALU.mult)


def CP(out, in_):
    return lambda e: e.tensor_copy(out=out, in_=in_)


def RCP(out, in_):
    return lambda e: e.reciprocal(out=out, in_=in_)


def MSET(out, v):
    return lambda e: e.memset(out, v)


def DMA(out, in_):
    return lambda e: e.dma_start(out=out, in_=in_)


class _Stop(Exception):
    pass


def build(NSTEP, debug=False, stop_after=None):
    NB = 8 * NSTEP + 1
    L = 128 * NB
    NG = 2 * NSTEP
    NOWN = NSTEP * 640

    nc = bass.Bass("TRN2", target_bir_lowering=False)
    dt = nc.dram_tensor
    h = dt("h", [L, D], F32, kind="ExternalInput").ap()
    hown = dt("hown", [NOWN, D], F32, kind="ExternalInput").ap()
    wcat = dt("wcat", [D, WC], F32, kind="ExternalInput").ap()
    wb2 = dt("wb2", [2560, D], F32, kind="ExternalInput").ap()
    gains = dt("gains", [2, D], F32, kind="ExternalInput").ap()
    sinks = dt("sinks", [2, 8], F32, kind="ExternalInput").ap()
    tabk = dt("tabk", [2, 128, NOWN], F32, kind="ExternalInput").ap()
    masks_in = dt("masks", [128, 8 * 512 + 3 * 128], F32, kind="ExternalInput").ap()
    consts_in = dt("consts", [128, 5 * 128], F32, kind="ExternalInput").ap()
    padb_in = dt("padb", [128, 2], F32, kind="ExternalInput").ap()
    y = dt("y", [NSTEP * 512, D], F32, kind="ExternalOutput").ap()

    okind = "ExternalOutput" if debug else "Internal"
    KTs = dt("KTs", [4, 128, L], BF16, kind=okind).ap()
    Vs = dt("Vs", [4, 128, NB * 128], BF16, kind=okind).ap()
    wcat_bf = dt("wcat_bf", [D, WC], BF16, kind="Internal").ap()
    wb2_bf = dt("wb2_bf", [2560, D], BF16, kind="Internal").ap()

    P = Prog()
    es = ExitStack()
    with es:
        def sb(name, shape, dtype):
            return es.enter_context(nc.sbuf_tensor(name, shape, dtype))

        def ps(name, shape, dtype=F32):
            return es.enter_context(nc.psum_tensor(name, shape, dtype))

        ring = [sb("ring%d" % i, [128, 4096], BF16) for i in range(4)]
        kvr = [sb("kvr%d" % i, [128, 2048], BF16) for i in range(4)]
        xt = [sb("xt%d" % i, [128, D], F32) for i in range(3)]
        xnb = [sb("xnb%d" % i, [128, D], BF16) for i in range(2)]
        xnT = [sb("xnT%d" % i, [128, KC, 640], BF16) for i in range(2)]
        gain_bc = sb("gain_bc", [128, D], F32)
        fgain_bc = sb("fgain_bc", [128, D], F32)
        arena = sb("arena", [128, 8192], BF16)
        KTst = [arena[:, i * 2048:(i + 1) * 2048].rearrange("p (h n) -> p h n", h=4) for i in range(2)]
        Vst = [arena[:, 4096 + i * 2048: 4096 + (i + 1) * 2048].rearrange("p (j n) -> p j n", j=4) for i in range(2)]
        oT_sw = arena[:, :].bitcast(F32).rearrange("p (c n) -> p c n", c=8)
        ss = sb("ss", [128, 8], F32)
        rstd = sb("rstd", [128, 8], F32)
        qT2 = [sb("qT_%d" % i, [128, 4, 512], BF16) for i in range(2)]
        xnT_f = sb("xnT_f", [128, KC, 128], BF16)
        KTst_f = [sb("KTst_f%d" % i, [128, 4, 128], BF16) for i in range(2)]
        Vst_f = [sb("Vst_f%d" % i, [128, 512], BF16) for i in range(2)]
        arena2 = sb("arena2", [128, 20480], BF16)

        def a2(off_kb, shape, dtype):
            n = 1
            for d_ in shape[1:]:
                n *= d_
            nb16 = n * (2 if dtype == F32 else 1)
            v = arena2[:, off_kb * 512: off_kb * 512 + nb16]
            if dtype == F32:
                v = v.bitcast(F32)
            if len(shape) == 3:
                v = v.rearrange("p (a n) -> p a n", a=shape[1])
            return v
        Eb2 = [a2(0, [128, 2, 512], F32), a2(18, [128, 2, 512], F32)]
        Lb = [a2(4 + 2 * i, [128, 2, 512], BF16) for i in range(2)]
        Wb = [a2(8 + 2 * i, [128, 2, 512], BF16) for i in range(2)]
        Rb = a2(12, [128, 2, 512], BF16)
        Pb = a2(14, [128, 2048], BF16)
        oT_sb = sb("oT_sb", [128, 4, 512], F32)
        masks = sb("masks_sb", [128, 8, 512], BF16)
        msw = sb("msw", [128, 3, 128], BF16)
        cst = sb("cst", [128, 5, 128], BF16)
        ident, negU, negOnes, zeros, ones = (cst[:, i, :] for i in range(5))
        padb = sb("padb_sb", [128, 2], F32)
        es_t = sb("es_t", [128, 8], F32)
        qT_sw = sb("qT_sw", [128, 8, 512], BF16)
        kt_sw = sb("kt_sw", [128, 2, 640], BF16)
        v_sw = sb("v_sw", [128, 5, 128], BF16)
        tk = sb("tk", [128, 2, 640], F32)
        rt1 = sb("rt1", [128, 640], F32)
        rt2 = sb("rt2", [128, 640], F32)
        sg = [a2(2 * i, [128, 512], F32) for i in range(4)]
        t1 = [a2(8 + 2 * i, [128, 512], F32) for i in range(2)]
        mT = a2(12, [128, 8, 512], BF16)
        outt = [a2(20 + 4 * i, [128, D], F32) for i in range(2)]
        gT_sb = a2(28, [128, 4, 512], BF16)
        gT_sw = a2(32, [128, 8, 512], BF16)
        junk = a2(36, [128, D], BF16)

        pz = [ps("pz%d" % i, [128, 1024]) for i in range(3)]
        po = [ps("po%d" % i, [128, 512]) for i in range(2)]
        banks = [pz[0][:, 0:512], pz[0][:, 512:1024], pz[1][:, 0:512], pz[1][:, 512:1024],
                 pz[2][:, 0:512], pz[2][:, 512:1024], po[0][:, :], po[1][:, :]]
        bname = ["B%d" % i for i in range(8)]

        def dma(eng, out, in_, reads=(), writes=()):
            return P.add(eng, DMA(out, in_), reads=reads, writes=writes, dma=True)

        def stage(name):
            if stop_after == name:
                P.barrier()
                P.frozen = True

        dma("pool", wcat_bf[:, C_KV:C_KV + 1024], wcat[:, C_KV:C_KV + 1024], writes=["wbf_0"])
        dma("pool", cst[:].rearrange("p a b -> p (a b)"), consts_in[:, :], writes=["cst"])
        dma("pool", masks[:].rearrange("p a b -> p (a b)"), masks_in[:, 0:4096], writes=["masks"])
        dma("pool", msw[:].rearrange("p a b -> p (a b)"), masks_in[:, 4096:4096 + 384], writes=["msw"])
        dma("sp", gain_bc[:], gains[0].partition_broadcast(128), writes=["gain"])
        dma("sp", fgain_bc[:], gains[1].partition_broadcast(128), writes=["fgain"])
        dma("sp", padb[:], padb_in[:, :], writes=["padb"])
        dma("sp", es_t[0:64, :], sinks[0].partition_broadcast(64), writes=["es_a"])
        dma("sp", es_t[64:128, :], sinks[1].partition_broadcast(64), writes=["es_b"])
        for c0 in range(1024, WC, 1024):
            c1 = min(WC, c0 + 1024)
            dma("pool", wcat_bf[:, c0:c1], wcat[:, c0:c1], writes=["wbf_%d" % c0])
        for r0 in range(0, 2560, 512):
            dma("pool", wb2_bf[r0:r0 + 512, :], wb2[r0:r0 + 512, :], writes=["wb2_%d" % r0])

        def wres_for(c0, c1):
            return sorted({"wbf_%d" % ((c // 1024) * 1024) for c in range(c0, c1, 128)})

        P.add("act", ACT(es_t[:], es_t[:], AF.Exp), reads=["es_a", "es_b"], writes=["es_t"])

        wv = wcat_bf.rearrange("(c p) n -> p c n", p=128)

        def load_unit(slot, c0, ncol=512):
            view = ring[slot][:, 0:8 * ncol].rearrange("p (c n) -> p c n", c=8)
            dma("sp", view, wv[:, :, c0:c0 + ncol], reads=wres_for(c0, c0 + ncol), writes=["ring%d" % slot])
            return view

        def load_unit_b2(slot, r0, nfc, c0):
            view = ring[slot][:, 0:nfc * 512].rearrange("p (c n) -> p c n", c=nfc)
            src = wb2_bf[r0:r0 + 128 * nfc, :].rearrange("(c p) n -> p c n", p=128)[:, :, c0:c0 + 512]
            rr = sorted({"wb2_%d" % ((r // 512) * 512) for r in range(r0, r0 + 128 * nfc, 128)})
            dma("sp", view, src, reads=rr, writes=["ring%d" % slot])
            return view

        stage("pro")
        xt_i = [0]

        def rms_scale(j, s):
            P.add("act", ACT(junk[:], xt[s][:], AF.Square, accum_out=ss[:, j:j + 1]),
                  reads=["xt%d" % s], writes=["junk", "ss%d" % j])
            P.add("act", ACT(rstd[:, j:j + 1], ss[:, j:j + 1], AF.Ln, scale=1.0 / D, bias=padb[:, 1:2]),
                  reads=["ss%d" % j, "padb"], writes=["rln%d" % j])
            P.add("act", ACT(rstd[:, j:j + 1], rstd[:, j:j + 1], AF.Exp, scale=-0.5),
                  reads=["rln%d" % j], writes=["rstd%d" % j])

        def norm_transpose(src_ap, nblk, xi):
            for j in range(nblk):
                s = xt_i[0] % 3
                xt_i[0] += 1
                dma("sp", xt[s][:], src_ap[j * 128:(j + 1) * 128, :], writes=["xt%d" % s])
                rms_scale(j, s)
                b = j % 2
                P.add("dve", STT(xnb[b][:], xt[s][:], rstd[:, j:j + 1], gain_bc[:], ALU.mult, ALU.mult),
                      reads=["xt%d" % s, "rstd%d" % j, "gain"], writes=["xnb%d" % b])
                bk = 6 + (j % 2)
                pb = banks[bk].bitcast(BF16)
                P.group("pe", [TR(pb[:, c * 128:(c + 1) * 128], xnb[b][:, c * 128:(c + 1) * 128], ident)
                               for c in range(KC)],
                        reads=["xnb%d" % b, "cst"], writes=[bname[bk]])
                P.add("act", ACT(xnT[xi][:, :, j * 128:(j + 1) * 128],
                                 pb[:, 0:1024].rearrange("p (c n) -> p c n", c=KC), AF.Copy),
                      reads=[bname[bk]], writes=["xnT%d_%d" % (xi, j)])

        def xres(xi, j0, j1):
            return ["xnT%d_%d" % (xi, j) for j in range(j0, j1)]

        evac_i = [0]

        def evac_copy(out_ap, in_ap, reads, writes, scale=None):
            k = evac_i[0]
            evac_i[0] += 1
            if k % 2 == 0:
                P.add("act", ACT(out_ap, in_ap, AF.Copy, scale=scale), reads=reads, writes=writes)
            elif scale is None:
                P.add("dve", CP(out_ap, in_ap), reads=reads, writes=writes)
            else:
                P.add("dve", TSMUL(out_ap, in_ap, scale), reads=reads, writes=writes)

        def proj_fm(w_view, col0, xT_ap, ntok, bk, reads):
            P.group("pe", [MM(banks[bk][:, 0:ntok], w_view[:, c, col0:col0 + 128], xT_ap[:, c, 0:ntok],
                              c == 0, c == KC - 1) for c in range(KC)],
                    reads=reads, writes=[bname[bk]])

        wkv0 = load_unit(0, C_KV)
        wkv1 = load_unit(1, C_KV + 512)
        groups = [(0, 1), (1, 4), (5, 4)] + ([(9, 4), (13, 4)] if NSTEP >= 2 else [])
        bki = [0]
        for gi, (b0, nblk) in enumerate(groups):
            ntok = nblk * 128
            xb = gi % 2
            st = gi % 2
            norm_transpose(h[b0 * 128:(b0 + nblk) * 128, :], nblk, xb)
            for hp in range(4):
                bk = bki[0] % 6
                bki[0] += 1
                proj_fm(wkv0, hp * 128, xnT[xb], ntok, bk, xres(xb, 0, nblk) + ["ring0"])
                evac_copy(KTst[st][:, hp, 0:ntok], banks[bk][:, 0:ntok], [bname[bk]], ["KTst%d_%d" % (st, hp)])
            for j in range(nblk):
                bk = bki[0] % 6
                bki[0] += 1
                P.group("pe", [MM(banks[bk][:, :], xnT[xb][:, c, j * 128:(j + 1) * 128], wkv1[:, c, :],
                                  c == 0, c == KC - 1) for c in range(KC)],
                        reads=xres(xb, j, j + 1) + ["ring1"], writes=[bname[bk]])
                evac_copy(Vst[st][:, j, :], banks[bk][:, :], [bname[bk]], ["Vst%d_%d" % (st, j)])
            t0 = b0 * 128
            dma("pool", KTs[:, :, t0:t0 + ntok].rearrange("h p t -> p h t"), KTst[st][:, :, 0:ntok],
                reads=["KTst%d_%d" % (st, hp) for hp in range(4)],
                writes=["KTs_b%d" % b for b in range(b0, b0 + nblk)])
            for hp in range(4):
                dma("pool", Vs[hp, :, t0:t0 + ntok].rearrange("p (j c) -> p j c", c=128),
                    Vst[st][:, 0:nblk, hp * 128:(hp + 1) * 128],
                    reads=["Vst%d_%d" % (st, j) for j in range(nblk)],
                    writes=["Vs_b%d_%d" % (b, hp) for b in range(b0, b0 + nblk)])

        stage("p1")

        def blk_group(kb):
            return 0 if kb == 0 else (kb - 1) // 4 + 1

        kv_i = [0]

        def fgroup(fns, reads, writes, split=True):
            if not split or len(fns) <= 4:
                P.group("pe", fns, reads=reads, writes=writes)
                return
            hlf = len(fns) // 2
            P.group("pe", fns[:hlf], reads=reads, writes=writes)
            yield
            P.group("pe", fns[hlf:], reads=reads, writes=writes)

        def kv_filler(m):
            dq = "pool"
            blocks = list(range(8 * m + 9, 8 * m + 17))
            w0 = ring[0][:, 0:4096].rearrange("p (c n) -> p c n", c=8)
            w1 = ring[1][:, 0:4096].rearrange("p (c n) -> p c n", c=8)
            dma(dq, w0, wv[:, :, C_KV:C_KV + 512], reads=wres_for(C_KV, C_KV + 512), writes=["ring0"])
            dma(dq, w1, wv[:, :, C_KV + 512:C_KV + 1024], reads=wres_for(C_KV + 512, C_KV + 1024), writes=["ring1"])
            slots = {}

            def xload(i):
                s = xt_i[0] % 3
                xt_i[0] += 1
                slots[i] = s
                b = blocks[i]
                dma(dq, xt[s][:], h[b * 128:(b + 1) * 128, :], writes=["xt%d" % s])

            xload(0)
            xload(1)
            yield
            bk = 7
            pb = banks[bk].bitcast(BF16)
            for i, b in enumerate(blocks):
                s = slots[i]
                st = i % 2
                rms_scale(5, s)
                P.add("dve", STT(xnb[st][:], xt[s][:], rstd[:, 5:6], gain_bc[:], ALU.mult, ALU.mult),
                      reads=["xt%d" % s, "rstd5", "gain"], writes=["xnb%d" % st])
                if i + 2 < len(blocks):
                    xload(i + 2)
                P.group("pe", [TR(pb[:, c * 128:(c + 1) * 128], xnb[st][:, c * 128:(c + 1) * 128], ident)
                               for c in range(KC)],
                        reads=["xnb%d" % st, "cst"], writes=[bname[bk]])
                P.add("dve", CP(xnT_f[:, :, :], pb[:, 0:1024].rearrange("p (c n) -> p c n", c=KC)),
                      reads=[bname[bk]], writes=["xnT_f"])
                yield
                for hh in range(2):
                    fns = []
                    for q_ in range(2):
                        hp = 2 * hh + q_
                        fns += [MM(banks[bk][:, q_ * 128:(q_ + 1) * 128], w0[:, c, hp * 128:(hp + 1) * 128],
                                   xnT_f[:, c, :], c == 0, c == KC - 1) for c in range(KC)]
                    yield from fgroup(fns, ["xnT_f", "ring0"], [bname[bk]])
                    P.add("dve", CP(KTst_f[st][:, 2 * hh:2 * hh + 2, :],
                                    banks[bk][:, 0:256].rearrange("p (a n) -> p a n", a=2)),
                          reads=[bname[bk]], writes=["KTst_f%d_%d" % (st, hh)])
                    yield
                yield from fgroup([MM(banks[bk][:, :], xnT_f[:, c, :], w1[:, c, :], c == 0, c == KC - 1)
                                   for c in range(KC)], ["xnT_f", "ring1"], [bname[bk]])
                P.add("dve", CP(Vst_f[st][:, :], banks[bk][:, :]), reads=[bname[bk]], writes=["Vst_f%d" % st])
                dma(dq, KTs[:, :, b * 128:(b + 1) * 128].rearrange("h p t -> p h t"), KTst_f[st][:, :, :],
                    reads=["KTst_f%d_0" % st, "KTst_f%d_1" % st], writes=["KTs_b%d" % b])
                for hp in range(4):
                    dma(dq, Vs[hp, :, b * 128:(b + 1) * 128], Vst_f[st][:, hp * 128:(hp + 1) * 128],
                        reads=["Vst_f%d" % st], writes=["Vs_b%d_%d" % (b, hp)])
                yield

        def front_end(m, banks_ok, dq, dve_only):
            par = m % 2
            own = hown[m * 640:(m + 1) * 640, :]
            xo = xnT[par]
            xq = xo[:, :, 128:640]
            xq_res = xres(par, 1, 5)
            xo_res = xres(par, 0, 5)
            bi = [0]

            def nb():
                b = banks_ok[bi[0] % len(banks_ok)]
                bi[0] += 1
                return b

            def ev(out_ap, in_ap, reads, writes, scale=None):
                if not dve_only:
                    evac_copy(out_ap, in_ap, reads, writes, scale)
                elif scale is None:
                    P.add("dve", CP(out_ap, in_ap), reads=reads, writes=writes)
                else:
                    P.add("dve", TSMUL(out_ap, in_ap, scale), reads=reads, writes=writes)

            def lu(slot, c0, ncol=512):
                view = ring[slot][:, 0:8 * ncol].rearrange("p (c n) -> p c n", c=8)
                dma(dq, view, wv[:, :, c0:c0 + ncol], reads=wres_for(c0, c0 + ncol), writes=["ring%d" % slot])
                return view

            slots = {}

            def xload(j):
                s = xt_i[0] % 3
                xt_i[0] += 1
                slots[j] = s
                dma(dq, xt[s][:], own[j * 128:(j + 1) * 128, :], writes=["xt%d" % s])

            xload(0)
            xload(1)
            wq = lu(0, C_Q)
            wk = lu(1, C_SWK)
            wvv = lu(2, C_SWV, ncol=128)
            wa0 = lu(3, C_SWQA)
            dma(dq, tk[:, 0, :], tabk[0, :, m * 640:(m + 1) * 640], writes=["tk0"])
            dma(dq, tk[:, 1, :], tabk[1, :, m * 640:(m + 1) * 640], writes=["tk1"])
            yield
            for j in range(5):
                s = slots[j]
                rms_scale(j, s)
                b = j % 2
                P.add("dve", STT(xnb[b][:], xt[s][:], rstd[:, j:j + 1], gain_bc[:], ALU.mult, ALU.mult),
                      reads=["xt%d" % s, "rstd%d" % j, "gain"], writes=["xnb%d" % b])
                if j + 2 < 5:
                    xload(j + 2)
                bk = nb()
                pb = banks[bk].bitcast(BF16)
                P.group("pe", [TR(pb[:, c * 128:(c + 1) * 128], xnb[b][:, c * 128:(c + 1) * 128], ident)
                               for c in range(KC)],
                        reads=["xnb%d" % b, "cst"], writes=[bname[bk]])
                ev(xo[:, :, j * 128:(j + 1) * 128], pb[:, 0:1024].rearrange("p (c n) -> p c n", c=KC),
                   [bname[bk]], ["xnT%d_%d" % (par, j)])
                yield
            for hp in range(4):
                bk = nb()
                yield from fgroup([MM(banks[bk][:, 0:512], wq[:, c, hp * 128:(hp + 1) * 128], xq[:, c, 0:512],
                                      c == 0, c == KC - 1) for c in range(KC)],
                                  xq_res + ["ring0"], [bname[bk]], dve_only)
                ev(qT2[par][:, hp, :], banks[bk][:, :], [bname[bk]], ["qT%d_%d" % (par, hp)], scale=0.125)
                yield
            wb0 = lu(0, C_SWQB)
            for g in range(2):
                for (n0, n1) in ((0, 512), (512, 640)):
                    for which, rt, tname in ((0, rt1, "tk0"), (1, rt2, "tk1")):
                        col = which * 256 + g * 128
                        bk = nb()
                        yield from fgroup([MM(banks[bk][:, 0:n1 - n0], wk[:, c, col:col + 128], xo[:, c, n0:n1],
                                              c == 0, c == KC - 1) for c in range(KC)],
                                          xo_res + ["ring1"], [bname[bk]], dve_only and n1 - n0 > 128)
                        P.add("dve", TT(rt[:, n0:n1], banks[bk][:, 0:n1 - n0], tk[:, which, n0:n1], ALU.mult),
                              reads=[bname[bk], tname], writes=["rt%d" % (which + 1)])
                        if which == 1:
                            P.add("dve", TT(kt_sw[:, g, n0:n1], rt1[:, n0:n1], rt2[:, n0:n1], ALU.add),
                                  reads=["rt1", "rt2"], writes=["kt_sw"])
                        yield
            wa1 = lu(1, C_SWQA + 512)
            bk = nb()
            for j in range(4):
                P.group("pe", [MM(banks[bk][:, j * 128:(j + 1) * 128], xo[:, c, j * 128:(j + 1) * 128], wvv[:, c, :],
                                  c == 0, c == KC - 1) for c in range(KC)],
                        reads=xo_res + ["ring2"], writes=[bname[bk]])
            ev(v_sw[:, 0:4, :], banks[bk][:, :].rearrange("p (j c) -> p j c", j=4), [bname[bk]], ["v_sw_a"])
            yield
            bk = nb()
            P.group("pe", [MM(banks[bk][:, 0:128], xo[:, c, 512:640], wvv[:, c, :], c == 0, c == KC - 1)
                           for c in range(KC)],
                    reads=xo_res + ["ring2"], writes=[bname[bk]])
            ev(v_sw[:, 4, :], banks[bk][:, 0:128], [bname[bk]], ["v_sw_b"])
            yield
            wb1 = lu(2, C_SWQB + 512)
            for half, (wa, ra, wb_, rb) in enumerate(((wa0, "ring3", wb0, "ring0"), (wa1, "ring1", wb1, "ring2"))):
                for cc in range(4):
                    cq = half * 4 + cc
                    for (wvw, rr, rt, which) in ((wa, ra, rt1, 0), (wb_, rb, rt2, 1)):
                        bk = nb()
                        yield from fgroup([MM(banks[bk][:, 0:512], wvw[:, c, cc * 128:(cc + 1) * 128],
                                              xq[:, c, 0:512], c == 0, c == KC - 1) for c in range(KC)],
                                          xq_res + [rr], [bname[bk]], dve_only)
                        P.add("dve", STT(rt[:, 0:512], banks[bk][:, :], 0.125, tk[:, which, 128:640],
                                         ALU.mult, ALU.mult),
                              reads=[bname[bk], "tk%d" % which], writes=["rt%d" % (which + 1)])
                        if which == 1:
                            P.add("dve", TT(qT_sw[:, cq, :], rt1[:, 0:512], rt2[:, 0:512], ALU.add),
                                  reads=["rt1", "rt2"], writes=["qT_sw%d" % cq])
                        yield

        for _ in front_end(0, list(range(8)), "sp", False):
            pass
        P.barrier()

        for m in range(NSTEP):
            par = m % 2
            xo = xnT[par]
            xq = xo[:, :, 128:640]
            xq_res = xres(par, 1, 5)
            own = hown[m * 640:(m + 1) * 640, :]
            qT = qT2[par]
            stage("proj")
            for g in range(2):
                for j in range(4):
                    for kb2 in range(2):
                        fns = []
                        for h8 in range(8):
                            cq = 4 * g + h8 // 2
                            hf = h8 % 2
                            zc = hf * 512 + (h8 // 2) * 128
                            fns.append(MM(pz[kb2][:, zc:zc + 128],
                                          kt_sw[hf * 64:(hf + 1) * 64, g, (j + kb2) * 128:(j + kb2 + 1) * 128],
                                          qT_sw[hf * 64:(hf + 1) * 64, cq, j * 128:(j + 1) * 128],
                                          h8 < 2, True, True))
                        mi = (2 if (m == 0 and j == 0) else 0) if kb2 == 0 else 1
                        for h8 in range(8):
                            fns.append(MM(pz[kb2][:, h8 * 128:(h8 + 1) * 128], ident, msw[:, mi, :], False, True, True))
                        P.group("pe", fns, reads=["kt_sw", "msw", "cst"] + ["qT_sw%d" % (4 * g + a) for a in range(4)],
                                writes=["B%d" % (2 * kb2), "B%d" % (2 * kb2 + 1)])
                        P.add("act", ACT(Pb[:, kb2 * 1024:(kb2 + 1) * 1024], pz[kb2][:, :], AF.Exp),
                              reads=["B%d" % (2 * kb2), "B%d" % (2 * kb2 + 1)], writes=["Pb%d" % kb2])
                    stage("swa1")
                    for which, bk in ((0, 4), (1, 5)):
                        fns = []
                        for hf in range(2):
                            for kb2 in range(2):
                                lhs = v_sw[:, j + kb2, g * 64:(g + 1) * 64] if which == 0 else ones[:, 0:64]
                                rhs = Pb[:, kb2 * 1024 + hf * 512:kb2 * 1024 + (hf + 1) * 512].rearrange(
                                    "p (a n) -> p a n", a=4)
                                fns.append(MM(banks[bk][hf * 64:(hf + 1) * 64, :].rearrange("p (a n) -> p a n", a=4),
                                              lhs, rhs, kb2 == 0, kb2 == 1))
                        P.group("pe", fns, reads=["Pb0", "Pb1", "v_sw_a", "v_sw_b", "cst"], writes=[bname[bk]])
                    stage("swa2")
                    P.add("dve", TT(rt1[:, 0:512].rearrange("p (a n) -> p a n", a=4),
                                    banks[5][:, :].rearrange("p (a n) -> p a n", a=4),
                                    es_t[:, 4 * g:4 * g + 4].unsqueeze(2).to_broadcast([128, 4, 128]), ALU.add),
                          reads=["B5", "es_t"], writes=["rt1"])
                    P.add("act", ACT(rt2[:, 0:512], rt1[:, 0:512], AF.Ln), reads=["rt1"], writes=["rt2"])
                    P.add("act", ACT(rt2[:, 0:512], rt2[:, 0:512], AF.Exp, scale=-1.0), reads=["rt2"], writes=["rt2"])
                    P.add("dve", TT(oT_sw[:, 4 * g:4 * g + 4, j * 128:(j + 1) * 128],
                                    banks[4][:, :].rearrange("p (a n) -> p a n", a=4),
                                    rt2[:, 0:512].rearrange("p (a n) -> p a n", a=4), ALU.mult),
                          reads=["B4", "rt2"], writes=["oT_sw"])

            stage("swa")
            its = []
            kmax = 8 * m + 8
            for hp in range(4):
                for kb in range(kmax, -1, -1):
                    rel = kb - 8 * m
                    its.append(dict(hp=hp, kb=kb, first=(kb == kmax), last=(kb == 0),
                                    c0=(128 * (rel - 5) if rel >= 5 else 0),
                                    mi=(rel - 1 if rel >= 1 else None)))
            loads = []
            for i, it in enumerate(its):
                ch = 0 if it["kb"] == 0 else (it["kb"] - 1) // 8 + 1
                it["kl"] = 0 if it["kb"] == 0 else (it["kb"] - 1) % 8
                if not loads or loads[-1][1:] != (it["hp"], ch):
                    loads.append((i, it["hp"], ch))
                it["ld"] = len(loads) - 1
            last_it = {}
            for i, it in enumerate(its):
                last_it[it["ld"]] = i
            slot_of = {}
            nl = [0]

            def issue_loads(t_done):
                while nl[0] < len(loads) and (nl[0] < 4 or last_it[nl[0] - 4] <= t_done):
                    _, hp, ch = loads[nl[0]]
                    slot = kv_i[0] % 4
                    kv_i[0] += 1
                    slot_of[nl[0]] = slot
                    tb0, nblk = (0, 1) if ch == 0 else (8 * (ch - 1) + 1, 8)
                    gr = ["KTs_b%d" % b for b in range(tb0, tb0 + nblk)]
                    gv = ["Vs_b%d_%d" % (b, hp) for b in range(tb0, tb0 + nblk)]
                    dma("sp", kvr[slot][:, 0:nblk * 128], KTs[hp, :, tb0 * 128:(tb0 + nblk) * 128],
                        reads=gr, writes=["kvK%d" % slot])
                    dma("sp", kvr[slot][:, 1024:1024 + nblk * 128], Vs[hp, :, tb0 * 128:(tb0 + nblk) * 128],
                        reads=gv, writes=["kvV%d" % slot])
                    nl[0] += 1

            def zres(t):
                k = t % 3
                return ["B%d" % (2 * k), "B%d" % (2 * k + 1)]

            def zv(t, c0):
                return pz[t % 3][:, :].rearrange("p (e n) -> p e n", e=2)[:, :, c0:512]

            def st_QK(t):
                it = its[t]
                slot = slot_of[it["ld"]]
                c0, kl, hp = it["c0"], it["kl"], it["hp"]
                fns = [MM(pz[t % 3][:, e_ * 512 + c0:(e_ + 1) * 512],
                          kvr[slot][e_ * 64:(e_ + 1) * 64, kl * 128:(kl + 1) * 128],
                          qT[e_ * 64:(e_ + 1) * 64, hp, c0:512], True, True, True) for e_ in range(2)]
                if it["mi"] is not None:
                    fns += [MM(pz[t % 3][:, e_ * 512 + c0:(e_ + 1) * 512], ident, masks[:, it["mi"], c0:512],
                               False, True, True) for e_ in range(2)]
                P.group("pe", fns, reads=["kvK%d" % slot, "qT%d_%d" % (par, hp), "masks", "cst"],
                        writes=zres(t))

            def st_E(t):
                it = its[t]
                c0 = it["c0"]
                bias = padb[:, 0:1] if it["kb"] == 0 else None
                P.add("act", ACT(Eb2[t % 2][:, :, c0:512], zv(t, c0), AF.Exp, bias=bias),
                      reads=zres(t) + ["padb"], writes=["Eb%d" % (t % 2)])

            def st_L(t):
                c0 = its[t]["c0"]
                P.add("act", ACT(Lb[t % 2][:, :, c0:512], Eb2[t % 2][:, :, c0:512], AF.Ln, bias=1.0),
                      reads=["Eb%d" % (t % 2)], writes=["Lb%d" % (t % 2)])

            def st_ADD(t):
                it = its[t]
                c0 = it["c0"]
                if it["first"]:
                    P.add("dve", MSET(Rb[:], 0.0), writes=["Rb"])
                if it["last"]:
                    return
                P.add("dve", TT(Rb[:, :, c0:512], Rb[:, :, c0:512], Lb[t % 2][:, :, c0:512], ALU.add),
                      reads=["Lb%d" % (t % 2), "Rb"], writes=["Rb"])

            def st_CS(t):
                it = its[t]
                c0 = it["c0"]
                fns = []
                for e_ in range(2):
                    o = pz[t % 3][:, e_ * 512 + c0:(e_ + 1) * 512]
                    fns.append(MM(o, negU, Lb[t % 2][:, e_, c0:512], False, True, True))
                    if not it["first"]:
                        fns.append(MM(o, negOnes, Rb[:, e_, c0:512], False, True, True))
                P.group("pe", fns, reads=["Lb%d" % (t % 2), "cst"] + ([] if it["first"] else ["Rb"]),
                        writes=zres(t))

            def st_W(t):
                it = its[t]
                c0 = it["c0"]
                bias = padb[:, 0:1] if it["kb"] == 0 else None
                P.add("act", ACT(Wb[t % 2][:, :, c0:512], zv(t, c0), AF.Exp, bias=bias),
                      reads=zres(t) + ["padb"], writes=["Wb%d" % (t % 2)])

            def st_PV(t):
                it = its[t]
                c0, kl, hp = it["c0"], it["kl"], it["hp"]
                slot = slot_of[it["ld"]]
                ob = 0
                fns = []
                for e_ in range(2):
                    vcol = 1024 + kl * 128 + e_ * 64
                    fns.append(MM(po[ob][e_ * 64:(e_ + 1) * 64, c0:512], kvr[slot][:, vcol:vcol + 64],
                                  Wb[t % 2][:, e_, c0:512], it["first"], True, True))
                P.group("pe", fns, reads=["Wb%d" % (t % 2), "kvV%d" % slot, "masks", "cst"], writes=["B6"])
                if it["last"]:
                    P.add("dve", CP(oT_sb[:, hp, :], po[ob][:, :]), reads=["B6"], writes=["oT_sb%d" % hp])

            n = len(its)
            issue_loads(-1)
            def both(m_):
                if m_ >= 1:
                    yield from kv_filler(m_)
                yield from front_end(m_ + 1, [7], "pool", True)
            filler = both(m) if m + 1 < NSTEP else None
            n_items = 140 if m >= 1 else 72
            stride = max(1, n // n_items)
            per_tick = 1
            pre = {}

            def prefetch_backend():
                pre["wz"] = load_unit(0, C_ZSB)
                pre["wzw"] = [load_unit(1, C_ZSW), load_unit(2, C_ZSW + 512)]
                pre["wbs0"] = load_unit_b2(3, 0, 4, 0)
            if filler is None:
                prefetch_backend()
            for t in range(-3, n):
                if 0 <= t + 2 < n:
                    st_E(t + 2)
                if 0 <= t + 1 < n:
                    st_CS(t + 1)
                    st_ADD(t + 1)
                if 0 <= t < n:
                    st_W(t)
                if 0 <= t + 3 < n:
                    st_QK(t + 3)
                if 0 <= t < n:
                    st_PV(t)
                    issue_loads(t)
                if 0 <= t + 2 < n:
                    st_L(t + 2)
                if filler is not None and t >= 0 and t % stride == 0:
                    try:
                        for _ in range(per_tick):
                            next(filler)
                    except StopIteration:
                        filler = None
                        prefetch_backend()
            if filler is not None:
                for _ in filler:
                    pass
                prefetch_backend()

            P.barrier()
            stage("sb")
            wz = pre["wz"]
            wzw = pre["wzw"]
            for ci in range(12):
                if ci < 4:
                    wvw, rr, col = wz, "ring0", ci * 128
                    osrc, odst, ores, dres = oT_sb[:, ci, :], gT_sb[:, ci, :], "oT_sb%d" % ci, "gT_sb%d" % ci
                else:
                    cq = ci - 4
                    wvw, rr, col = wzw[cq // 4], "ring%d" % (1 + cq // 4), (cq % 4) * 128
                    osrc, odst, ores, dres = oT_sw[:, cq, :], gT_sw[:, cq, :], "oT_sw", "gT_sw%d" % cq
                bk = ci % 4
                proj_fm(wvw, col, xq, 512, bk, xq_res + [rr])
                P.add("act", ACT(sg[bk][:], banks[bk][:, :], AF.Sigmoid), reads=[bname[bk]], writes=["sg%d" % bk])
                P.add("dve", TT(sg[bk][:], sg[bk][:], banks[bk][:, :], ALU.mult),
                      reads=[bname[bk], "sg%d" % bk], writes=["sg%d" % bk])
                P.add("dve", TT(odst, sg[bk][:], osrc, ALU.mult), reads=["sg%d" % bk, ores], writes=[dres])

            stage("gate")
            for hh in range(2):
                wbs = pre["wbs0"] if hh == 0 else load_unit_b2(3, 0, 4, hh * 512)
                wbw = load_unit_b2(0, 512, 8, hh * 512)
                wgs = load_unit(1, C_GSB + hh * 512)
                wgw = load_unit(2, C_GSW + hh * 512)
                for cc in range(4):
                    c_out = hh * 4 + cc
                    q4 = (cc % 2) * 4
                    b_ys, b_yw, b_gs, b_gw = q4, q4 + 1, q4 + 2, q4 + 3
                    P.group("pe", [MM(banks[b_ys][:, :], wbs[:, fc, cc * 128:(cc + 1) * 128], gT_sb[:, fc, :],
                                      fc == 0, fc == 3) for fc in range(4)],
                            reads=["ring3"] + ["gT_sb%d" % k for k in range(4)], writes=[bname[b_ys]])
                    P.group("pe", [MM(banks[b_yw][:, :], wbw[:, fc, cc * 128:(cc + 1) * 128], gT_sw[:, fc, :],
                                      fc == 0, fc == 7) for fc in range(8)],
                            reads=["ring0"] + ["gT_sw%d" % k for k in range(8)], writes=[bname[b_yw]])
                    proj_fm(wgs, cc * 128, xq, 512, b_gs, xq_res + ["ring1"])
                    proj_fm(wgw, cc * 128, xq, 512, b_gw, xq_res + ["ring2"])
                    ia, ib = (cc % 2) * 2, (cc % 2) * 2 + 1
                    sA, sB, rA, rB = sg[ia], sg[ib], "sg%d" % ia, "sg%d" % ib
                    tt, rtt = t1[cc % 2], "t1_%d" % (cc % 2)
                    P.add("act", ACT(sA[:], banks[b_gs][:, :], AF.Sigmoid), reads=[bname[b_gs]], writes=[rA])
                    P.add("act", ACT(sB[:], banks[b_gw][:, :], AF.Sigmoid), reads=[bname[b_gw]], writes=[rB])
                    P.add("dve", TT(tt[:], sA[:], banks[b_ys][:, :], ALU.mult), reads=[rA, bname[b_ys]], writes=[rtt])
                    P.add("dve", TT(sB[:], sB[:], banks[b_yw][:, :], ALU.mult), reads=[rB, bname[b_yw]], writes=[rB])
                    P.add("dve", TT(mT[:, c_out, :], tt[:], sB[:], ALU.add), reads=[rB, rtt], writes=["mT%d" % c_out])

            stage("merge")
            wo = [load_unit_b2(3, 1536, 8, 0), load_unit_b2(0, 1536, 8, 512)]
            wor = ["ring3", "ring0"]
            for j in range(4):
                s = xt_i[0] % 3
                xt_i[0] += 1
                dma("sp", xt[s][:], own[128 + j * 128:128 + (j + 1) * 128, :], writes=["xt%d" % s])
                for hh in range(2):
                    bk = (j % 2) * 2 + hh
                    P.group("pe", [MM(banks[bk][:, :], mT[:, c, j * 128:(j + 1) * 128], wo[hh][:, c, :],
                                      c == 0, c == KC - 1) for c in range(KC)],
                            reads=[wor[hh]] + ["mT%d" % k for k in range(8)], writes=[bname[bk]])
                    P.add("dve", TT(xt[s][:, hh * 512:(hh + 1) * 512], xt[s][:, hh * 512:(hh + 1) * 512],
                                    banks[bk][:, :], ALU.add),
                          reads=[bname[bk], "xt%d" % s], writes=["xt%d" % s])
                rms_scale(j, s)
                ob = j % 2
                P.add("dve", STT(outt[ob][:], xt[s][:], rstd[:, j:j + 1], fgain_bc[:], ALU.mult, ALU.mult),
                      reads=["xt%d" % s, "rstd%d" % j, "fgain"], writes=["outt%d" % ob])
                r0 = m * 512 + j * 128
                dma("pool", y[r0:r0 + 128, :], outt[ob][:], reads=["outt%d" % ob], writes=["y_%d_%d" % (m, j)])
            P.barrier()

        sem_stack = ExitStack()
        with sem_stack:
            def semctx(name):
                return sem_stack.enter_context(nc.semaphore(name))
            state = {}
            _prepare(P, semctx, state)
            with nc.Block() as block:
                block.tensor(lambda eng: _emit_engine(P, "pe", eng, state))
                block.scalar(lambda eng: _emit_engine(P, "act", eng, state))
                block.vector(lambda eng: _emit_engine(P, "dve", eng, state))
                block.gpsimd(lambda eng: _emit_engine(P, "pool", eng, state))
                block.sync(lambda eng: _emit_engine(P, "sp", eng, state))
    build.last_stats = dict(n_inst={e: len(P.q[e]) for e in P.ENGS}, maxcount=state["maxcount"])
    return nc


def _host_prep(x, meta_tokens, norm_gain, w_in, w_branch_sb, w_branch_swa, w_out, attn_sinks, final_norm_gain):
    B, SEQ, _ = x.shape
    NSTEP = SEQ // 1024
    NB = 8 * NSTEP + 1
    f32 = np.float32
    w = np.asarray(w_in[0], f32)
    offs = np.cumsum([0, 512, 512, 512, 1024, 128, 128, 512, 1024, 1024, 1024])
    sb_q, sb_k, sb_v, sw_q, sw_k, sw_v, sb_z, sw_z, g_sb, g_sw = [w[:, offs[i]:offs[i + 1]] for i in range(10)]
    perm64 = np.concatenate([np.arange(32, 64), np.arange(0, 32)])
    kA = np.concatenate([sw_k[:, 0:64], sw_k[:, 0:64], sw_k[:, 64:128], sw_k[:, 64:128]], axis=1)
    kB0 = sw_k[:, 0:64][:, perm64]
    kB1 = sw_k[:, 64:128][:, perm64]
    kB = np.concatenate([kB0, kB0, kB1, kB1], axis=1)
    qperm = np.concatenate([hh * 64 + perm64 for hh in range(16)])
    wcat = np.concatenate([sb_k, sb_v, sb_q, kA, kB, sw_v, sw_q, sw_q[:, qperm], sb_z, sw_z, g_sb, g_sw], axis=1)
    assert wcat.shape[1] == WC
    wcat = np.ascontiguousarray(wcat, f32)
    wb2 = np.ascontiguousarray(np.concatenate([w_branch_sb[0], w_branch_swa[0], w_out[0]], axis=0), f32)
    gains = np.ascontiguousarray(np.stack([norm_gain[0], final_norm_gain]), f32)
    sk = np.asarray(attn_sinks[0], f32)
    sinks = np.ascontiguousarray(np.stack([sk[0::2], sk[1::2]]), f32)
    half = 32
    inv = 10000.0 ** (-np.arange(half, dtype=np.float64) / half)
    dd = np.arange(128) % 64
    fi = dd % 32
    sign = np.where(dd < 32, -1.0, 1.0).astype(f32)

    def tables(pos):
        ang = pos.astype(np.float64)[None, :] * inv[fi][:, None]
        return np.cos(ang).astype(f32), (np.sin(ang) * sign[:, None]).astype(f32)

    consts = np.zeros((128, 5, 128), f32)
    consts[:, 0, :] = np.eye(128, dtype=f32)
    jj, s_ = np.meshgrid(np.arange(128), np.arange(128), indexing="ij")
    consts[:, 1, :] = np.where(jj >= s_, -1.0, 0.0)
    consts[:, 2, :] = -1.0
    consts[:, 3, :] = 0.0
    consts[:, 4, :] = 1.0
    consts = consts.reshape(128, 640)
    padb = np.zeros((128, 2), f32)
    padb[:PAD, 0] = NEG
    padb[:, 1] = EPS
    in_maps = []
    for b in range(B):
        hb = np.concatenate([np.zeros((PAD, D), f32), np.asarray(meta_tokens, f32), np.asarray(x[b], f32)], axis=0)
        for r in range(2):
            rows = []
            posk = []
            posq = []
            for m in range(NSTEP):
                first = 8 * m + 1 + 4 * r
                rows.append(hb[(first - 1) * 128:(first + 4) * 128])
                p = np.arange((first - 1) * 128, (first + 4) * 128) - PAD
                posk.append(p)
                posq.append(p[128:])
            hown = np.ascontiguousarray(np.concatenate(rows, axis=0))
            ck, sk_ = tables(np.concatenate(posk))
            tabk = np.ascontiguousarray(np.stack([ck, sk_]))
            mk = np.zeros((128, 8, 512), f32)
            s_l = np.arange(128)[:, None]
            t_l = np.arange(128)[None, :]
            tri = np.where(s_l < t_l, 0.0, NEG).astype(f32)
            for rel in range(1, 9):
                for qi in range(4):
                    qrel = 1 + 4 * r + qi
                    if rel < qrel:
                        blkm = 0.0
                    elif rel == qrel:
                        blkm = tri
                    else:
                        blkm = NEG
                    mk[:, rel - 1, qi * 128:(qi + 1) * 128] = blkm
            msw = np.zeros((128, 3, 128), f32)
            msw[:, 0, :] = np.where(s_l > t_l, 0.0, NEG)
            msw[:, 1, :] = np.where(s_l <= t_l, 0.0, NEG)
            msw[:, 2, :] = msw[:, 0, :]
            if r == 0:
                msw[:PAD, 2, :] = NEG
            masks = np.ascontiguousarray(np.concatenate([mk.reshape(128, 4096), msw.reshape(128, 384)], axis=1))
            in_maps.append(dict(h=np.ascontiguousarray(hb), hown=hown, wcat=wcat, wb2=wb2, gains=gains, sinks=sinks,
                                tabk=tabk, masks=masks, consts=consts, padb=padb))
    return NSTEP, in_maps


_NC_CACHE = {}


def run(x, meta_tokens, norm_gain, w_in, w_branch_sb, w_branch_swa, w_out, attn_sinks, final_norm_gain, debug=False,
        trace=False, stop_after=None):
    x = np.asarray(x)
    B, SEQ, _ = x.shape
    NSTEP, in_maps = _host_prep(x, meta_tokens, norm_gain, w_in, w_branch_sb, w_branch_swa, w_out, attn_sinks,
                                final_norm_gain)
    key = (NSTEP, debug, stop_after)
    if key not in _NC_CACHE:
        _NC_CACHE[key] = build(NSTEP, debug, stop_after)
    nc = _NC_CACHE[key]
    res = run_bass_kernel_spmd(nc, in_maps, core_ids=list(range(2 * B)), trace=trace)
    out = np.zeros((B, SEQ, D), np.float32)
    for b in range(B):
        for r in range(2):
            yy = res.results[2 * b + r]["y"]
            for m in range(NSTEP):
                first = 8 * m + 1 + 4 * r
                out[b, (first - 1) * 128:(first + 3) * 128] = yy[m * 512:(m + 1) * 512]
    if debug:
        return out, res
    return out


def kernel(x, meta_tokens, norm_gain, w_in, w_branch_sb, w_branch_swa, w_out, attn_sinks, final_norm_gain):
    return run(np.asarray(x), np.asarray(meta_tokens), np.asarray(norm_gain), np.asarray(w_in),
               np.asarray(w_branch_sb), np.asarray(w_branch_swa), np.asarray(w_out), np.asarray(attn_sinks),
               np.asarray(final_norm_gain))
```
